# Optimizing a Trainium2 kernel written in Bass

```python
import jax, jax.numpy as jnp
from jax import lax
import numpy as np

D_MODEL = 1024
BATCH = 16
SEQ = 256
DEPTH = 4
DEC_BATCH = 4
DEC_SEQ = 1024
PAST_LEN = 512

GRID_W = 64
N_MIXERS = 3
N_ATTN = (DEPTH + 2) // 3
N_CONV = (DEPTH + 1) // 3
N_GLA = DEPTH // 3
NA_HEADS = 16
NA_HEAD_DIM = D_MODEL // NA_HEADS
NA_WIN_R = 8
NA_WIN_C = 16
CONV_WIDTH = 31
GLA_HEADS = 4
GLA_KEY_DIM = D_MODEL // 2
GLA_VAL_DIM = D_MODEL
GLA_DK = GLA_KEY_DIM // GLA_HEADS
GLA_DV = GLA_VAL_DIM // GLA_HEADS
GLA_GATE_RANK = 16
GLA_GATE_NORM = 16.0
GLA_CHUNK = 32
D_FF = 2816
FFN_CONV_WIDTH = 3
NORM_EPS = 1e-6
NEG_INF = -1e30

kernel_name = "hybrid_diffusion_prefix_trunk_step"


def rms_norm(x, g):
    xf = x.astype(jnp.float32)
    y = xf * lax.rsqrt(jnp.mean(xf * xf, axis=-1, keepdims=True) + NORM_EPS)
    return (y * g.astype(jnp.float32)).astype(x.dtype)


def layer_norm(x, g, b):
    xf = x.astype(jnp.float32)
    mu = jnp.mean(xf, axis=-1, keepdims=True)
    var = jnp.mean(jnp.square(xf - mu), axis=-1, keepdims=True)
    y = (xf - mu) * lax.rsqrt(var + NORM_EPS)
    return (y * g.astype(jnp.float32) + b.astype(jnp.float32)).astype(x.dtype)


def dwconv1d(x, w, b):
    ch = x.shape[-1]
    y = lax.conv_general_dilated(x, w[:, None, :].astype(x.dtype), (1,), 'SAME',
                                 dimension_numbers=('NWC', 'WIO', 'NWC'),
                                 feature_group_count=ch)
    return y + b.astype(x.dtype)


def adaln(cond, w, b):
    mod = jax.nn.silu(cond) @ w + b
    return [m[:, None, :] for m in jnp.split(mod, 6, axis=-1)]


def modulate(x, shift, scale):
    return x * (1 + scale) + shift


def _qkv(h, w_qkv, q_g, k_g):
    b, l, _ = h.shape
    qkv = (h @ w_qkv).reshape(b, l, 3, NA_HEADS, NA_HEAD_DIM)
    q = rms_norm(qkv[:, :, 0], q_g) * (NA_HEAD_DIM ** -0.5)
    k = rms_norm(qkv[:, :, 1], k_g)
    return q, k, qkv[:, :, 2]


def attn_context(h, w_qkv, w_o, q_g, k_g):
    b, l, _ = h.shape
    q, k, v = _qkv(h, w_qkv, q_g, k_g)
    s = jnp.einsum('bqhd,bkhd->bhqk', q, k, preferred_element_type=jnp.float32)
    p = jax.nn.softmax(s, axis=-1).astype(v.dtype)
    o = jnp.einsum('bhqk,bkhd->bqhd', p, v).reshape(b, l, D_MODEL)
    return o @ w_o, k, v


def na_latent(h, ctx_k, ctx_v, w_qkv, w_o, q_g, k_g, rpb):
    b, l, _ = h.shape
    rows = l // GRID_W
    wr = min(NA_WIN_R, rows)
    q, k, v = _qkv(h, w_qkv, q_g, k_g)
    qg = q.reshape(b, rows, GRID_W, NA_HEADS, NA_HEAD_DIM)
    kg = k.reshape(b, rows, GRID_W, NA_HEADS, NA_HEAD_DIM)
    vg = v.reshape(b, rows, GRID_W, NA_HEADS, NA_HEAD_DIM)
    r_idx = jnp.arange(rows)
    r_start = jnp.clip(r_idx - wr // 2, 0, rows - wr)
    row_idx = r_start[:, None] + jnp.arange(wr)[None, :]
    k_blk = kg[:, row_idx]
    v_blk = vg[:, row_idx]
    s_loc = jnp.einsum('brqhd,brikhd->bhrqik', qg, k_blk,
                       preferred_element_type=jnp.float32)
    cols = jnp.arange(GRID_W)
    c_start = jnp.clip(cols - NA_WIN_C // 2, 0, GRID_W - NA_WIN_C)
    in_win = (cols[None, :] >= c_start[:, None]) & (cols[None, :] < c_start[:, None] + NA_WIN_C)
    dr = row_idx - r_idx[:, None] + (NA_WIN_R - 1)
    dc = jnp.clip(cols[None, :] - cols[:, None] + (NA_WIN_C - 1), 0, 2 * NA_WIN_C - 2)
    bias = rpb.astype(jnp.float32)[:, dr[:, None, :, None], dc[None, :, None, :]]
    bias = jnp.where(in_win[None, None, :, None, :], bias, NEG_INF)
    s_loc = (s_loc + bias[None]).reshape(b, NA_HEADS, rows, GRID_W, wr * GRID_W)
    s_ctx = jnp.einsum('brqhd,bchd->bhrqc', qg, ctx_k, preferred_element_type=jnp.float32)
    p = jax.nn.softmax(jnp.concatenate([s_loc, s_ctx], axis=-1), axis=-1).astype(v.dtype)
    p_loc = p[..., :wr * GRID_W].reshape(b, NA_HEADS, rows, GRID_W, wr, GRID_W)
    p_ctx = p[..., wr * GRID_W:]
    o = (jnp.einsum('bhrqik,brikhd->brqhd', p_loc, v_blk)
         + jnp.einsum('bhrqc,bchd->brqhd', p_ctx, ctx_v))
    return o.reshape(b, l, D_MODEL) @ w_o


def conformer_conv(h, w_pw1, b_pw1, w_dw, b_dw, ln_g, ln_b, w_pw2, b_pw2):
    a, g = jnp.split(h @ w_pw1 + b_pw1, 2, axis=-1)
    u = a * jax.nn.sigmoid(g)
    u = dwconv1d(u, w_dw, b_dw)
    u = jax.nn.silu(layer_norm(u, ln_g, ln_b))
    return u @ w_pw2 + b_pw2


def gla_chunk_scan(q, k, v, g, s0):
    b, l, h, dk = q.shape
    dv = v.shape[-1]
    n = l // GLA_CHUNK
    def chunks(t):
        return t.reshape(b, n, GLA_CHUNK, h, t.shape[-1]).transpose(1, 0, 3, 2, 4).astype(jnp.float32)
    qc, kc, vc, gc = chunks(q), chunks(k), chunks(v), chunks(g)
    bcum = jnp.cumsum(gc, axis=3)
    b_last = bcum[..., -1:, :]
    causal = jnp.tril(jnp.ones((GLA_CHUNK, GLA_CHUNK), dtype=bool))
    diff = bcum[..., :, None, :] - bcum[..., None, :, :]
    decay = jnp.exp(jnp.where(causal[:, :, None], diff, -jnp.inf))
    att = jnp.einsum('nbhtd,nbhsd,nbhtsd->nbhts', qc, kc, decay)
    o_intra = jnp.einsum('nbhts,nbhsv->nbhtv', att, vc)
    q_in = qc * jnp.exp(bcum)
    k_out = kc * jnp.exp(b_last - bcum)
    def step(s, xs):
        qi, ki, vi, dl = xs
        o = jnp.einsum('bhtd,bhdv->bhtv', qi, s)
        s = s * dl[:, :, 0, :, None] + jnp.einsum('bhtd,bhtv->bhdv', ki, vi)
        return s, o
    s_fin, o_inter = lax.scan(step, s0.astype(jnp.float32), (q_in, k_out, vc, jnp.exp(b_last)))
    o = (o_intra + o_inter).transpose(1, 0, 3, 2, 4).reshape(b, l, h, dv)
    return o.astype(v.dtype), s_fin.astype(v.dtype)


def gla_mixer(h, s0_f, s0_b, w_q, w_k, w_v, w_g, w_gk1, w_gk2, b_gk, o_g, w_o):
    b, l, _ = h.shape
    q = (h @ w_q).reshape(b, l, GLA_HEADS, GLA_DK) * (GLA_DK ** -0.5)
    k = (h @ w_k).reshape(b, l, GLA_HEADS, GLA_DK)
    v = (h @ w_v).reshape(b, l, GLA_HEADS, GLA_DV)
    def log_gate(d):
        z = (h @ w_gk1[d]) @ w_gk2[d] + b_gk[d]
        return (jax.nn.log_sigmoid(z.astype(jnp.float32)) / GLA_GATE_NORM).reshape(b, l, GLA_HEADS, GLA_DK)
    flip = lambda t: jnp.flip(t, axis=1)
    o_f, st_f = gla_chunk_scan(q, k, v, log_gate(0), s0_f)
    o_b, st_b = gla_chunk_scan(flip(q), flip(k), flip(v), flip(log_gate(1)), s0_b)
    o = rms_norm(o_f + flip(o_b), o_g) * jax.nn.silu((h @ w_g).reshape(b, l, GLA_HEADS, GLA_DV))
    return o.reshape(b, l, GLA_VAL_DIM) @ w_o, st_f, st_b


def conv_ffn(h, w_up, b_up, w_dw, b_dw, w_down, b_down):
    u = dwconv1d(h @ w_up + b_up, w_dw, b_dw)
    a, g = jnp.split(u, 2, axis=-1)
    return (jax.nn.silu(g) * a) @ w_down + b_down


def setup_inputs(seed: int = 0) -> dict:
    key = jax.random.key(seed)
    ks = iter(jax.random.split(key, 40))
    def nrm(shape, scale):
        return jax.random.normal(next(ks), shape, jnp.float32) * scale
    D = D_MODEL
    inp = {}
    inp['x_prompt'] = nrm((BATCH, SEQ, D), 1.0)
    inp['x_sample'] = nrm((DEC_BATCH, DEC_SEQ, D), 1.0)
    inp['c'] = nrm((DEC_BATCH, D), 1.0)
    inp['cache_attn_k'] = nrm((DEC_BATCH, N_ATTN, PAST_LEN, NA_HEADS, NA_HEAD_DIM), 1.0)
    inp['cache_attn_v'] = nrm((DEC_BATCH, N_ATTN, PAST_LEN, NA_HEADS, NA_HEAD_DIM), 1.0)
    inp['state_gla_fwd'] = nrm((DEC_BATCH, N_GLA, GLA_HEADS, GLA_DK, GLA_DV), 1.0)
    inp['state_gla_bwd'] = nrm((DEC_BATCH, N_GLA, GLA_HEADS, GLA_DK, GLA_DV), 1.0)
    inp['c_ctx'] = nrm((D,), 1.0)
    inp['mod_w'] = nrm((DEPTH, D, 6 * D), 0.5 * D ** -0.5)
    inp['mod_b'] = nrm((DEPTH, 6 * D), 0.01)
    inp['norm1_g'] = 1.0 + nrm((DEPTH, D), 0.02)
    inp['norm2_g'] = 1.0 + nrm((DEPTH, D), 0.02)
    inp['attn_w_qkv'] = nrm((N_ATTN, D, 3 * D), D ** -0.5)
    inp['attn_w_o'] = nrm((N_ATTN, D, D), D ** -0.5)
    inp['attn_q_norm'] = 1.0 + nrm((N_ATTN, NA_HEAD_DIM), 0.02)
    inp['attn_k_norm'] = 1.0 + nrm((N_ATTN, NA_HEAD_DIM), 0.02)
    inp['attn_rpb'] = nrm((N_ATTN, NA_HEADS, 2 * NA_WIN_R - 1, 2 * NA_WIN_C - 1), 0.1)
    inp['conv_w_pw1'] = nrm((N_CONV, D, 2 * D), D ** -0.5)
    inp['conv_b_pw1'] = nrm((N_CONV, 2 * D), 0.01)
    inp['conv_w_dw'] = nrm((N_CONV, CONV_WIDTH, D), CONV_WIDTH ** -0.5)
    inp['conv_b_dw'] = nrm((N_CONV, D), 0.01)
    inp['conv_ln_g'] = 1.0 + nrm((N_CONV, D), 0.02)
    inp['conv_ln_b'] = nrm((N_CONV, D), 0.01)
    inp['conv_w_pw2'] = nrm((N_CONV, D, D), D ** -0.5)
    inp['conv_b_pw2'] = nrm((N_CONV, D), 0.01)
    inp['gla_w_q'] = nrm((N_GLA, D, GLA_KEY_DIM), D ** -0.5)
    inp['gla_w_k'] = nrm((N_GLA, D, GLA_KEY_DIM), D ** -0.5)
    inp['gla_w_v'] = nrm((N_GLA, D, GLA_VAL_DIM), D ** -0.5)
    inp['gla_w_g'] = nrm((N_GLA, D, GLA_VAL_DIM), D ** -0.5)
    inp['gla_w_gk1'] = nrm((N_GLA, 2, D, GLA_GATE_RANK), D ** -0.5)
    inp['gla_w_gk2'] = nrm((N_GLA, 2, GLA_GATE_RANK, GLA_KEY_DIM), GLA_GATE_RANK ** -0.5)
    inp['gla_b_gk'] = nrm((N_GLA, 2, GLA_KEY_DIM), 0.1)
    inp['gla_o_norm'] = 1.0 + nrm((N_GLA, GLA_DV), 0.02)
    inp['gla_w_o'] = nrm((N_GLA, GLA_VAL_DIM, D), GLA_VAL_DIM ** -0.5)
    inp['ffn_w_up'] = nrm((DEPTH, D, 2 * D_FF), D ** -0.5)
    inp['ffn_b_up'] = nrm((DEPTH, 2 * D_FF), 0.01)
    inp['ffn_w_dw'] = nrm((DEPTH, FFN_CONV_WIDTH, 2 * D_FF), FFN_CONV_WIDTH ** -0.5)
    inp['ffn_b_dw'] = nrm((DEPTH, 2 * D_FF), 0.01)
    inp['ffn_w_down'] = nrm((DEPTH, D_FF, D), D_FF ** -0.5)
    inp['ffn_b_down'] = nrm((DEPTH, D), 0.01)
    return inp


def reference(x_prompt, x_sample, c, cache_attn_k, cache_attn_v, state_gla_fwd, state_gla_bwd, c_ctx,
              mod_w, mod_b, norm1_g, norm2_g,
              attn_w_qkv, attn_w_o, attn_q_norm, attn_k_norm, attn_rpb,
              conv_w_pw1, conv_b_pw1, conv_w_dw, conv_b_dw, conv_ln_g, conv_ln_b, conv_w_pw2, conv_b_pw2,
              gla_w_q, gla_w_k, gla_w_v, gla_w_g, gla_w_gk1, gla_w_gk2, gla_b_gk, gla_o_norm, gla_w_o,
              ffn_w_up, ffn_b_up, ffn_w_dw, ffn_b_dw, ffn_w_down, ffn_b_down):
    xp, xs = x_prompt, x_sample
    new_k, new_v, new_sf, new_sb = [], [], [], []
    for i in range(DEPTH):
        kind, j = i % N_MIXERS, i // N_MIXERS
        sh1p, sc1p, g1p, sh2p, sc2p, g2p = adaln(c_ctx[None, :], mod_w[i], mod_b[i])
        sh1s, sc1s, g1s, sh2s, sc2s, g2s = adaln(c, mod_w[i], mod_b[i])
        hp = modulate(rms_norm(xp, norm1_g[i]), sh1p, sc1p)
        hs = modulate(rms_norm(xs, norm1_g[i]), sh1s, sc1s)
        if kind == 0:
            op, kp, vp = attn_context(hp, attn_w_qkv[j], attn_w_o[j], attn_q_norm[j], attn_k_norm[j])
            os_ = na_latent(hs, cache_attn_k[:, j], cache_attn_v[:, j], attn_w_qkv[j], attn_w_o[j],
                            attn_q_norm[j], attn_k_norm[j], attn_rpb[j])
            new_k.append(kp)
            new_v.append(vp)
        elif kind == 1:
            cw = (conv_w_pw1[j], conv_b_pw1[j], conv_w_dw[j], conv_b_dw[j], conv_ln_g[j], conv_ln_b[j],
                  conv_w_pw2[j], conv_b_pw2[j])
            op = conformer_conv(hp, *cw)
            os_ = conformer_conv(hs, *cw)
        else:
            gw = (gla_w_q[j], gla_w_k[j], gla_w_v[j], gla_w_g[j], gla_w_gk1[j], gla_w_gk2[j], gla_b_gk[j],
                  gla_o_norm[j], gla_w_o[j])
            zero = jnp.zeros((xp.shape[0], GLA_HEADS, GLA_DK, GLA_DV), xp.dtype)
            op, sfp, sbp = gla_mixer(hp, zero, zero, *gw)
            os_, _, _ = gla_mixer(hs, state_gla_fwd[:, j], state_gla_bwd[:, j], *gw)
            new_sf.append(sfp)
            new_sb.append(sbp)
        xp = xp + g1p * op
        xs = xs + g1s * os_
        fw = (ffn_w_up[i], ffn_b_up[i], ffn_w_dw[i], ffn_b_dw[i], ffn_w_down[i], ffn_b_down[i])
        xp = xp + g2p * conv_ffn(modulate(rms_norm(xp, norm2_g[i]), sh2p, sc2p), *fw)
        xs = xs + g2s * conv_ffn(modulate(rms_norm(xs, norm2_g[i]), sh2s, sc2s), *fw)
    new_attn_k = jnp.stack(new_k, axis=1)
    new_attn_v = jnp.stack(new_v, axis=1)
    new_gla_fwd = jnp.stack(new_sf, axis=1)
    new_gla_bwd = jnp.stack(new_sb, axis=1)
    return (xp, xs, new_attn_k, new_attn_v, new_gla_fwd, new_gla_bwd)
```

```python
from contextlib import ExitStack
import numpy as np
import concourse.bass as bass
import concourse.mybir as mybir
from concourse.bass_utils import run_bass_kernel_spmd

F32 = mybir.dt.float32
BF16 = mybir.dt.bfloat16
AF = mybir.ActivationFunctionType
ALU = mybir.AluOpType
AX = mybir.AxisListType

D = 1024
NT = 1024
DFF = 2816
NPAIR = DFF // 128
DEPTH = 4
EPS = 1e-6


class Buf:
    __slots__ = ("name", "w", "r", "dsem", "dcnt")

    def __init__(self, name):
        self.name = name
        self.w = None
        self.r = {}
        self.dsem = None
        self.dcnt = 0


class Eng:
    def __init__(self, name, h, sem):
        self.name = name
        self.h = h
        self.sem = sem
        self.seq = 0
        self.waited = {}


class KB:
    def __init__(self, nc, es):
        self.nc = nc
        self.es = es
        self.eng = {}
        for n, h in (("pe", nc.tensor), ("act", nc.scalar), ("dve", nc.vector), ("pool", nc.gpsimd), ("sp", nc.sync)):
            self.eng[n] = Eng(n, h, es.enter_context(nc.semaphore("sem_" + n)))
        self.nbuf = 0
        self.final = []
        self.semcache = {}

    def buf(self, name=None):
        self.nbuf += 1
        return Buf(name or ("b%d" % self.nbuf))

    def _waits(self, e, reads, writes):
        need = {}

        def add(ev):
            if ev is None:
                return
            s, v = ev
            if need.get(s, (None, 0))[1] < v:
                need[s] = (s, v)

        for b in reads:
            add(b.w)
        for b in writes:
            add(b.w)
            for s, v in b.r.items():
                add((s, v))
        for s, v in need.values():
            if e.name == "pe" and s is e.sem:
                continue
            if e.waited.get(s, 0) < v:
                e.h.wait_ge(s, v)
                e.waited[s] = v

    def op(self, en, fn, reads=(), writes=(), inc=True):
        e = self.eng[en]
        assert inc or en == "pe"
        self._waits(e, reads, writes)
        ins = fn(e.h)
        val = e.seq + 1
        if inc:
            ins.then_inc(e.sem, 1)
            e.seq += 1
        ev = (e.sem, val)
        for b in reads:
            if b.r.get(e.sem, 0) < val:
                b.r[e.sem] = val
        for b in writes:
            b.w = ev
            b.r = {}
        return ins

    def dma(self, en, out, in_, reads=(), writes=(), final=False):
        e = self.eng[en]
        self._waits(e, reads, writes)
        owner = writes[0] if writes else reads[0]
        st = self.semcache.get(owner.name)
        if st is None:
            st = [self.es.enter_context(self.nc.semaphore("ds_%s" % owner.name)), 0]
            self.semcache[owner.name] = st
        st[1] += 16
        owner.dsem, owner.dcnt = st[0], st[1]
        e.h.dma_start(out=out, in_=in_).then_inc(owner.dsem, 16)
        ev = (owner.dsem, owner.dcnt)
        for b in reads:
            if b.r.get(owner.dsem, 0) < owner.dcnt:
                b.r[owner.dsem] = owner.dcnt
        for b in writes:
            b.w = ev
            b.r = {}
        if final:
            self.final.append((en, ev))

    def finish(self):
        for en, (s, v) in self.final:
            e = self.eng[en]
            if e.waited.get(s, 0) < v:
                e.h.wait_ge(s, v)
                e.waited[s] = v


def build_program(cfg):
    nc = bass.Bass("TRN2", target_bir_lowering=False)
    mixers = cfg.get("mixers", (0, 1, 2))
    nlayers = cfg.get("nlayers", DEPTH)
    dbg = cfg.get("dbg", False)
    astage = cfg.get("astage", 3)

    def din(name, shape):
        return nc.dram_tensor(name, list(shape), F32, kind="ExternalInput").ap()

    def dout(name, shape):
        return nc.dram_tensor(name, list(shape), F32, kind="ExternalOutput").ap()

    xin = din("xin", (NT, D))
    cond = din("cond", (8, 128))
    flag = din("flag", (128, 1))
    identd = din("ident", (128, 128))
    mod_w = din("mod_w", (DEPTH, D, 6 * D))
    mod_b = din("mod_b", (DEPTH, 6 * D))
    norm1_g = din("norm1_g", (DEPTH, D))
    norm2_g = din("norm2_g", (DEPTH, D))
    ffn_w_up = din("ffn_w_up", (DEPTH, D, 2 * DFF))
    ffn_b_up = din("ffn_b_up", (DEPTH, 2 * DFF))
    ffn_w_dw = din("ffn_w_dw", (DEPTH, 3, 2 * DFF))
    ffn_b_dw = din("ffn_b_dw", (DEPTH, 2 * DFF))
    ffn_w_down = din("ffn_w_down", (DEPTH, DFF, D))
    ffn_b_down = din("ffn_b_down", (DEPTH, D))
    conv_w_pw1 = din("conv_w_pw1", (1, D, 2 * D))
    conv_b_pw1 = din("conv_b_pw1", (1, 2 * D))
    conv_w_dw = din("conv_w_dw", (1, 31, D))
    conv_b_dw = din("conv_b_dw", (1, D))
    conv_ln_g = din("conv_ln_g", (1, D))
    conv_ln_b = din("conv_ln_b", (1, D))
    conv_w_pw2 = din("conv_w_pw2", (1, D, D))
    conv_b_pw2 = din("conv_b_pw2", (1, D))
    attn_w_qkv = din("attn_w_qkv", (2, D, 3 * D))
    attn_w_o = din("attn_w_o", (2, D, D))
    attn_q_norm = din("attn_q_norm", (2, 64))
    attn_k_norm = din("attn_k_norm", (2, 64))
    rpbr_t = nc.dram_tensor("rpbr", [2, 16, 15, 127], F32, kind="ExternalInput")
    cmask = din("cmask", (128, 64))
    j2d = din("j2", (128, 128))
    blkd = din("blk", (128, 128))
    rowsel_d = nc.dram_tensor("rowsel", [16, 1024], BF16, kind="ExternalInput").ap()
    pmadd_d = nc.dram_tensor("pmadd", [16, 1024], BF16, kind="ExternalInput").ap()
    cmk = din("cmk", (128, 1))
    ck_d = din("ck", (2, 512, D))
    cv_d = din("cv", (2, 512, D))
    nk_d = dout("nk", (2, NT, D))
    nv_d = dout("nv", (2, NT, D))
    gla_w_q = din("gla_w_q", (1, D, 512))
    gla_w_k = din("gla_w_k", (1, D, 512))
    gla_w_v = din("gla_w_v", (1, D, D))
    gla_w_g = din("gla_w_g", (1, D, D))
    gla_w_gk1 = din("gla_w_gk1", (1, 2, D, 16))
    gla_w_gk2 = din("gla_w_gk2", (1, 2, 16, 512))
    gla_b_gk = din("gla_b_gk", (1, 2, 512))
    gla_o_norm = din("gla_o_norm", (1, 256))
    gla_w_o = din("gla_w_o", (1, D, D))
    s0f_d = din("s0f", (4, 128, 256))
    s0b_d = din("s0b", (4, 128, 256))
    maskf_d = din("maskf", (128, 128))
    maskb_d = din("maskb", (128, 128))
    gsf_d = dout("gsf", (4, 4, 128, 256))
    gsb_d = dout("gsb", (4, 4, 128, 256))
    yout = dout("yout", (NT, D))

    with ExitStack() as es:
        K = KB(nc, es)

        def sb(name, shape, dt):
            return es.enter_context(nc.sbuf_tensor(name, list(shape), dt))

        XT = sb("XT", (128, 8, NT), F32)
        XTb = [[K.buf("XT%d_%d" % (k, h)) for h in range(2)] for k in range(8)]
        HT = sb("HT", (128, 8, NT), BF16)
        HTb = [[K.buf("HT%d_%d" % (k, h)) for h in range(2)] for k in range(8)]
        MT = sb("MT", (128, NPAIR, NT), BF16)
        MTb = [K.buf("MT%d" % k) for k in range(NPAIR)]
        WK = [sb("WK%d" % i, (128, NT), F32) for i in range(8)]
        WKb = [K.buf("WK%d" % i) for i in range(8)]
        NWR = 3
        WR = [sb("WR%d" % i, (128, 6144), BF16) for i in range(NWR)]
        WRb = [K.buf("WR%d" % i) for i in range(NWR)]
        wr_next = [0]
        IDENT = sb("IDENT", (128, 128), F32)
        IDb = K.buf("CONSTS")
        ONESB = sb("ONESB", (128, 128), BF16)
        ONb = K.buf("ONESB")
        ONE1 = sb("ONE1", (1, 2), F32)
        O1b = K.buf("ONE1")
        FLAG = sb("FLAG", (128, 1), F32)
        FLb = IDb
        STG = [sb("STG%d" % i, (128, 128), F32) for i in range(3)]
        STGb = [K.buf("STG%d" % i) for i in range(3)]
        NVT = 8 + 8 + 48 + 44 * 5 + 8 + 8 + 8 + 8 + 44 * 2 + 8
        VT = sb("VT", (128, 1024), F32)
        VTb = K.buf("VT")
        SC = sb("SC", (128, 8), BF16)
        SCb = K.buf("SC")
        CNDT = sb("CNDT", (128, 8), F32)
        RSTD = sb("RSTD", (128, NT), F32)
        RSb = [K.buf("RSTD%d" % h) for h in range(2)]
        SQ = [sb("SQ%d" % i, (128, 512), BF16) for i in range(4)]
        SQb = [K.buf("SQ%d" % i) for i in range(4)]
        PAIR2 = sb("PAIR2", (128, 5632 + 960), BF16)
        PAIR2_bufs = []
        EBT2 = sb("EBT2", (128, 2, 1984), BF16)
        sq_next = [0]
        PS = [es.enter_context(nc.psum_tensor("PS%d" % i, [128, 512], F32)) for i in range(8)]
        PSb = [K.buf("PS%d" % i) for i in range(8)]
        ps_next = [0]
        ps_range = [0, 7]

        ps_banks = [list(range(7))]

        def ps():
            banks = ps_banks[0]
            i = ps_next[0] % len(banks)
            ps_next[0] = i + 1
            b = banks[i]
            return PS[b], PSb[b]

        def wslot():
            i = wr_next[0]
            wr_next[0] = (i + 1) % NWR
            return WR[i], WRb[i]

        col = {}
        c0 = 0
        for nm, n in (("g1", 8), ("g2", 8), ("bdown", 8), ("bup", 44), ("bdw", 44), ("w0", 44), ("w1", 44),
                      ("w2", 44), ("mod", 192), ("modb", 48), ("A1", 8), ("A2", 8), ("g2bd", 8), ("fw0", 44), ("fw2", 44), ("g1b", 8), ("cv", 48), ("cw", 248), ("gv", 16)):
            col[nm] = c0
            c0 += n
        assert c0 <= 1024

        cur_mod = [0]

        def vt(nm, j=0, n=1):
            if nm == "mod":
                j = j + cur_mod[0]
            return VT[:, col[nm] + j: col[nm] + j + n]

        K.dma("sp", IDENT[:], identd[:, :], writes=[IDb])
        K.dma("sp", FLAG[:], flag[:, :], writes=[FLb])
        K.op("dve", lambda h: h.memset(ONESB[:], 1.0 / D), writes=[ONb])
        K.op("dve", lambda h: h.memset(ONE1[:], 1.0), writes=[O1b])

        for tt in range(8):
            st, stb = WK[tt % 4], WKb[tt % 4]
            K.dma("sp", st[:], xin[tt * 128:(tt + 1) * 128, :], writes=[stb])
            for g in range(2):
                p, pb = ps()
                for kk in range(4):
                    k = g * 4 + kk
                    K.op("pe", lambda h, k=k, kk=kk, p=p, st=st: h.transpose(p[:, kk * 128:(kk + 1) * 128], st[:, k * 128:(k + 1) * 128], IDENT[:]),
                         reads=[stb, IDb], writes=[pb], inc=(kk == 3))
                hf = tt // 4
                K.op("dve" if g == 0 else "act",
                     (lambda h, p=p, g=g, tt=tt: h.tensor_copy(XT[:, g * 4:(g + 1) * 4, tt * 128:(tt + 1) * 128], p[:].rearrange("p (k n) -> p k n", k=4))) if g == 0 else
                     (lambda h, p=p, g=g, tt=tt: h.copy(XT[:, g * 4:(g + 1) * 4, tt * 128:(tt + 1) * 128], p[:].rearrange("p (k n) -> p k n", k=4))),
                     reads=[pb], writes=[XTb[k][hf] for k in range(g * 4, g * 4 + 4)])

        K.dma("sp", STG[0][0:8, :], cond[:, :], writes=[STGb[0]])
        p, pb = ps()
        K.op("pe", lambda h: h.transpose(p[:, 0:8], STG[0][0:8, :], IDENT[0:8, 0:8]), reads=[STGb[0], IDb], writes=[pb])
        K.op("act", lambda h: h.activation(out=SC[:], in_=p[:, 0:8], func=AF.Silu), reads=[pb], writes=[SCb])

        def vec_rows(stg_i, row0, src_ap, n):
            K.dma("sp", STG[stg_i][row0:row0 + n, :], src_ap, writes=[STGb[stg_i]])

        def stg_to_vt(stg_i, nrows, names):
            p, pb = ps()
            K.op("pe", lambda h: h.transpose(p[:, 0:nrows], STG[stg_i][0:nrows, :], IDENT[0:nrows, 0:nrows]),
                 reads=[STGb[stg_i], IDb], writes=[pb])
            c = col[names[0]]
            K.op("dve", lambda h: h.tensor_copy(VT[:, c:c + nrows], p[:, 0:nrows]), reads=[pb], writes=[VTb])


        def mod_compute(i, parts=(0, 1)):
            vec_rows(2, 64, mod_b[i].rearrange("(c p) -> c p", p=128), 48)
            mw = mod_w[i].rearrange("(k p) n -> p k n", p=128)
            pm, pmb = PS[7], PSb[7]
            wv = None
            for cg in [g_ for pt_ in parts for g_ in range(6 * pt_, 6 * pt_ + 6)]:
                w, wb = wslot()
                wv = w[:, 0:4096].rearrange("p (k n) -> p k n", k=8)
                K.dma("pool", wv, mw[:, :, cg * 512:(cg + 1) * 512], writes=[wb])
                off = 0
                for jj in range(4):
                    jcol = cg * 4 + jj
                    for k in range(8):
                        K.op("pe", lambda h, k=k, jj=jj, jcol=jcol, wv=wv, off=off: h.matmul(pm[:, jcol:jcol + 1], wv[:, k, off + jj * 128:off + (jj + 1) * 128], SC[:, k:k + 1], start=(k == 0), stop=(k == 7)),
                             reads=[SCb, wb], writes=[pmb], inc=(k == 7))
            p, pb = ps()
            K.op("pe", lambda h, p=p: h.transpose(p[:, 0:48], STG[2][64:112, :], IDENT[64:112, 64:112]), reads=[STGb[2], IDb], writes=[pb])
            cmb = col["modb"]
            K.op("act", lambda h, p=p: h.copy(VT[:, cmb:cmb + 48], p[:, 0:48]), reads=[pb], writes=[VTb])
            cm = col["mod"] + 48 * i
            c_lo, c_hi = 24 * min(parts), 24 * max(parts) + 24
            K.op("dve", lambda h: h.tensor_tensor(out=VT[:, cm + c_lo:cm + c_hi], in0=pm[:, c_lo:c_hi], in1=VT[:, cmb + c_lo:cmb + c_hi], op=ALU.add), reads=[pmb, VTb], writes=[VTb])

        def layer_vectors(i):
            r = lambda ap, n: ap.rearrange("(c p) -> c p", p=128)
            vec_rows(0, 0, r(norm1_g[i], 8), 8)
            vec_rows(0, 8, r(norm2_g[i], 8), 8)
            vec_rows(0, 16, r(ffn_b_down[i], 8), 8)
            vec_rows(0, 24, r(ffn_b_up[i], 44), 44)
            vec_rows(0, 68, r(ffn_b_dw[i], 44), 44)
            stg_to_vt(0, 112, ["g1"])
            vec_rows(1, 0, r(ffn_w_dw[i, 0], 44), 44)
            vec_rows(1, 44, r(ffn_w_dw[i, 1], 44), 44)
            stg_to_vt(1, 88, ["w0"])
            vec_rows(2, 0, r(ffn_w_dw[i, 2], 44), 44)
            stg_to_vt(2, 44, ["w2"])
            cur_mod[0] = 48 * i
            K.op("dve", lambda h: h.scalar_tensor_tensor(out=vt("A1", 0, 8), in0=vt("mod", 8, 8), scalar=1.0, in1=vt("g1", 0, 8), op0=ALU.add, op1=ALU.mult), reads=[VTb], writes=[VTb])
            K.op("dve", lambda h: h.tensor_scalar(out=vt("fw0", 0, 44), in0=vt("w0", 0, 44), scalar1=FLAG[:, 0:1], scalar2=None, op0=ALU.mult), reads=[VTb, FLb], writes=[VTb])
            K.op("dve", lambda h: h.tensor_scalar(out=vt("fw2", 0, 44), in0=vt("w2", 0, 44), scalar1=FLAG[:, 0:1], scalar2=None, op0=ALU.mult), reads=[VTb, FLb], writes=[VTb])

        def norm_mod(Aname, shift_off):
            for hf in range(2):
                sl = slice(hf * 512, (hf + 1) * 512)
                p, pb = ps()
                for k in range(8):
                    qi = sq_next[0]
                    sq_next[0] = (qi + 1) % 4
                    K.op("act", lambda h, k=k, qi=qi: h.activation(out=SQ[qi][:], in_=XT[:, k, sl], func=AF.Square), reads=[XTb[k][hf]], writes=[SQb[qi]])
                    K.op("pe", lambda h, k=k, qi=qi, p=p: h.matmul(p[:], ONESB[:], SQ[qi][:], start=(k == 0), stop=(k == 7)),
                         reads=[ONb, SQb[qi]], writes=[pb], inc=True)
                K.op("act", lambda h, p=p: h.activation(out=RSTD[:, sl], in_=p[:], func=AF.Ln, bias=EPSV[:, 0:1]), reads=[pb, EPb], writes=[RSb[hf]])
                K.op("act", lambda h: h.activation(out=RSTD[:, sl], in_=RSTD[:, sl], func=AF.Exp, scale=-0.5), reads=[RSb[hf]], writes=[RSb[hf]])
                for k in range(8):
                    wi = k % 2
                    K.op("dve", lambda h, k=k, wi=wi: h.tensor_tensor(out=WK[wi][:, 0:512], in0=XT[:, k, sl], in1=RSTD[:, sl], op=ALU.mult),
                         reads=[XTb[k][hf], RSb[hf]], writes=[WKb[wi]])
                    K.op("act", lambda h, k=k, wi=wi: h.activation(out=HT[:, k, sl], in_=WK[wi][:, 0:512], func=AF.Identity, scale=vt(Aname, k), bias=vt("mod", shift_off + k)),
                         reads=[WKb[wi], VTb], writes=[HTb[k][hf]])

        EPSV = sb("EPSV", (128, 1), F32)
        EPb = K.buf("EPSV")
        K.op("dve", lambda h: h.memset(EPSV[:], EPS), writes=[EPb])

        def epilogue_tiles(tts):
            for tt in tts:
                hf = tt // 4
                st, stb = WK[4 + tt % 4], WKb[4 + tt % 4]
                for g in range(2):
                    p, pb = ps()
                    for kk in range(4):
                        k = g * 4 + kk
                        K.op("pe", lambda h, k=k, kk=kk, p=p, tt=tt: h.transpose(p[:, kk * 128:(kk + 1) * 128], XT[:, k, tt * 128:(tt + 1) * 128], IDENT[:]),
                             reads=[XTb[k][hf], IDb], writes=[pb], inc=(kk == 3))
                    if g == 0:
                        K.op("dve", lambda h, p=p, st=st: h.tensor_copy(st[:, 0:512], p[:]), reads=[pb], writes=[stb])
                    else:
                        K.op("act", lambda h, p=p, st=st: h.copy(st[:, 512:1024], p[:]), reads=[pb], writes=[stb])
                K.dma("sp", yout[tt * 128:(tt + 1) * 128, :], st[:], reads=[stb], final=True)

        def ffn(i):
            if i == 0:
                mod_compute(0, (1,))
            K.op("dve", lambda h: h.scalar_tensor_tensor(out=vt("A2", 0, 8), in0=vt("mod", 32, 8), scalar=1.0, in1=vt("g2", 0, 8), op0=ALU.add, op1=ALU.mult), reads=[VTb], writes=[VTb])
            K.op("dve", lambda h: h.tensor_tensor(out=vt("g2bd", 0, 8), in0=vt("mod", 40, 8), in1=vt("bdown", 0, 8), op=ALU.mult), reads=[VTb], writes=[VTb])
            norm_mod("A2", 24)
            wup = ffn_w_up[i].rearrange("(k p) n -> p k n", p=128)
            wv = None
            for pr in range(NPAIR):
                if pr % 3 == 0:
                    w, wb = wslot()
                    ng = min(3, NPAIR - pr)
                    wv = w[:, :].rearrange("p (k a n) -> p k a n", k=8, a=2)
                    K.dma("pool", wv[:, :, 0, 0:ng * 128], wup[:, :, pr * 128:(pr + ng) * 128], writes=[wb])
                    K.dma("pool", wv[:, :, 1, 0:ng * 128], wup[:, :, DFF + pr * 128:DFF + (pr + ng) * 128], writes=[wb])
                set_i = pr % 2
                YA, YAb = WK[set_i * 4], WKb[set_i * 4]
                YG, YGb = WK[set_i * 4 + 1], WKb[set_i * 4 + 1]
                UA, UAb = WK[set_i * 4 + 2], WKb[set_i * 4 + 2]
                UG, UGb = WK[set_i * 4 + 3], WKb[set_i * 4 + 3]
                o = (pr % 3) * 128
                for a, (Y, Yb, ch) in enumerate(((YA, YAb, pr), (YG, YGb, NPAIR + pr))):
                    for hf in range(2):
                        sl = slice(hf * 512, (hf + 1) * 512)
                        p, pb = ps()
                        for k in range(8):
                            K.op("pe", lambda h, k=k, p=p, a=a, o=o, wv=wv, sl=sl: h.matmul(p[:], wv[:, k, a, o:o + 128], HT[:, k, sl], start=(k == 0), stop=(k == 7)),
                                 reads=[wb, HTb[k][hf]], writes=[pb], inc=(k == 7))
                        K.op("act", lambda h, p=p, Y=Y, sl=sl, ch=ch: h.activation(out=Y[:, sl], in_=p[:], func=AF.Identity, bias=vt("bup", ch)),
                             reads=[pb, VTb], writes=[Yb])
                for (Y, Yb, U, Ub, ch) in ((YA, YAb, UA, UAb, pr), (YG, YGb, UG, UGb, NPAIR + pr)):
                    K.op("act", lambda h, Y=Y, U=U, ch=ch: h.activation(out=U[:], in_=Y[:], func=AF.Identity, scale=vt("w1", ch), bias=vt("bdw", ch)),
                         reads=[Yb, VTb], writes=[Ub])
                    Y3 = Y[:, :].rearrange("p (s t) -> p s t", t=256)
                    U3 = U[:, :].rearrange("p (s t) -> p s t", t=256)
                    K.op("dve", lambda h, Y3=Y3, U3=U3, ch=ch: h.scalar_tensor_tensor(out=U3[:, :, 1:256], in0=Y3[:, :, 0:255], scalar=vt("w0", ch), in1=U3[:, :, 1:256], op0=ALU.mult, op1=ALU.add),
                         reads=[Yb, VTb, Ub], writes=[Ub])
                    K.op("dve", lambda h, Y3=Y3, U3=U3, ch=ch: h.scalar_tensor_tensor(out=U3[:, :, 0:255], in0=Y3[:, :, 1:256], scalar=vt("w2", ch), in1=U3[:, :, 0:255], op0=ALU.mult, op1=ALU.add),
                         reads=[Yb, VTb, Ub], writes=[Ub])
                    K.op("dve", lambda h, Y3=Y3, U3=U3, ch=ch: h.scalar_tensor_tensor(out=U3[:, 1:4, 0], in0=Y3[:, 0:3, 255], scalar=vt("fw0", ch), in1=U3[:, 1:4, 0], op0=ALU.mult, op1=ALU.add),
                         reads=[Yb, VTb, Ub], writes=[Ub])
                    K.op("dve", lambda h, Y3=Y3, U3=U3, ch=ch: h.scalar_tensor_tensor(out=U3[:, 0:3, 255], in0=Y3[:, 1:4, 0], scalar=vt("fw2", ch), in1=U3[:, 0:3, 255], op0=ALU.mult, op1=ALU.add),
                         reads=[Yb, VTb, Ub], writes=[Ub])
                K.op("act", lambda h, UG=UG: h.activation(out=UG[:], in_=UG[:], func=AF.Silu), reads=[UGb], writes=[UGb])
                K.op("dve", lambda h, UA=UA, UG=UG, pr=pr: h.tensor_tensor(out=MT[:, pr, :], in0=UA[:], in1=UG[:], op=ALU.mult), reads=[UAb, UGb], writes=[MTb[pr]])
            if i + 1 < nlayers:
                mod_compute(i + 1)
            wdn = ffn_w_down[i].rearrange("(k p) n -> p k n", p=128)
            for hf in range(2):
                sl = slice(hf * 512, (hf + 1) * 512)
                for n2 in range(4):
                    w, wb = wslot()
                    wv = w[:, 0:NPAIR * 256].rearrange("p (k n) -> p k n", k=NPAIR)
                    K.dma("pool", wv, wdn[:, :, n2 * 256:(n2 + 1) * 256], writes=[wb])
                    for nn in range(2):
                        c = n2 * 2 + nn
                        p, pb = ps()
                        for k in range(NPAIR):
                            K.op("pe", lambda h, k=k, p=p, nn=nn, wv=wv, sl=sl: h.matmul(p[:], wv[:, k, nn * 128:(nn + 1) * 128], MT[:, k, sl], start=(k == 0), stop=(k == NPAIR - 1)),
                                 reads=[wb, MTb[k]], writes=[pb], inc=(k == NPAIR - 1))
                        wi = 2 + (c % 2)
                        K.op("act", lambda h, p=p, wi=wi, c=c: h.activation(out=WK[wi][:, 0:512], in_=p[:], func=AF.Identity, scale=vt("mod", 40 + c), bias=vt("g2bd", c)),
                             reads=[pb, VTb], writes=[WKb[wi]])
                        K.op("dve", lambda h, wi=wi, c=c, sl=sl: h.tensor_tensor(out=XT[:, c, sl], in0=XT[:, c, sl], in1=WK[wi][:, 0:512], op=ALU.add),
                             reads=[WKb[wi], XTb[c][hf]], writes=[XTb[c][hf]])
                if i == nlayers - 1:
                    epilogue_tiles(range(hf * 4, hf * 4 + 4))

        IDENTB = sb("IDENTB", (128, 128), BF16)
        IDBb = K.buf("IDENTB")
        K.op("dve", lambda h: h.tensor_copy(IDENTB[:], IDENT[:]), reads=[IDb], writes=[IDBb])
        MTflat = MT[:, :, :].rearrange("p k n -> p (k n)")
        MTf32 = MTflat.bitcast(F32)

        def alias(dst, src):
            ev = {}
            for b in src:
                if b.w is not None:
                    s_, v_ = b.w
                    ev[s_] = max(ev.get(s_, 0), v_)
                for s_, v_ in b.r.items():
                    ev[s_] = max(ev.get(s_, 0), v_)
            for b in dst:
                b.w = None
                b.r = dict(ev)

        def out_proj(wdram, kchunks, rhs_fn, rhs_bufs_fn, bias_name):
            wv_d = wdram.rearrange("(k p) n -> p k n", p=128)
            per = 8192 // (kchunks * 128) * 128
            per = min(per, 512)
            nslots = D // per
            for hf in range(2):
                sl = slice(hf * 512, (hf + 1) * 512)
                for sI in range(nslots):
                    w, wb = wslot()
                    wv = w[:, 0:kchunks * per].rearrange("p (k n) -> p k n", k=kchunks)
                    K.dma("pool", wv, wv_d[:, :, sI * per:(sI + 1) * per], writes=[wb])
                    for nn in range(per // 128):
                        c = sI * (per // 128) + nn
                        p, pb = ps()
                        for k in range(kchunks):
                            K.op("pe", lambda h, k=k, p=p, nn=nn, wv=wv, sl=sl: h.matmul(p[:], wv[:, k, nn * 128:(nn + 1) * 128], rhs_fn(k, sl), start=(k == 0), stop=(k == kchunks - 1)),
                                 reads=[wb] + rhs_bufs_fn(k, hf), writes=[pb], inc=(k == kchunks - 1))
                        wi = 2 + (c % 2)
                        K.op("act", lambda h, p=p, wi=wi, c=c: h.activation(out=WK[wi][:, 0:512], in_=p[:], func=AF.Identity, scale=vt("mod", 16 + c), bias=vt("g1b", c)),
                             reads=[pb, VTb], writes=[WKb[wi]])
                        K.op("dve", lambda h, wi=wi, c=c, sl=sl: h.tensor_tensor(out=XT[:, c, sl], in0=XT[:, c, sl], in1=WK[wi][:, 0:512], op=ALU.add),
                             reads=[WKb[wi], XTb[c][hf]], writes=[XTb[c][hf]])

        def conv_mixer(i, j):
            norm_mod("A1", 0)
            r = lambda ap: ap.rearrange("(c p) -> c p", p=128)
            vec_rows(0, 0, r(conv_b_pw1[j]), 16)
            vec_rows(0, 16, r(conv_b_dw[j]), 8)
            vec_rows(0, 24, r(conv_ln_g[j]), 8)
            vec_rows(0, 32, r(conv_ln_b[j]), 8)
            vec_rows(0, 40, r(conv_b_pw2[j]), 8)
            stg_to_vt(0, 48, ["cv"])
            K.op("dve", lambda h: h.tensor_tensor(out=vt("g1b", 0, 8), in0=vt("mod", 16, 8), in1=vt("cv", 40, 8), op=ALU.mult), reads=[VTb], writes=[VTb])
            UPW = 286
            UPb = [K.buf("UP%d" % c) for c in range(8)]
            TMb = [K.buf("TM%d" % t) for t in range(4)]
            alias(UPb + TMb, MTb)
            UPall = MTflat[:, 0:8 * 4 * UPW]
            UP = [MTflat[:, c * 4 * UPW:(c + 1) * 4 * UPW].rearrange("p (s t) -> p s t", s=4) for c in range(8)]
            tm0 = (8 * 4 * UPW * 2 + 3) // 4
            TM = [MTf32[:, tm0 + t * 1024: tm0 + (t + 1) * 1024] for t in range(4)]
            K.op("pool", lambda h: h.memset(UPall, 0.0), writes=UPb)
            K.dma("sp", TM[3][0:31, :], conv_w_dw[j], writes=[TMb[3]])
            p, pb = ps()
            for c in range(8):
                K.op("pe", lambda h, c=c, p=p: h.transpose(p[:, c * 31:(c + 1) * 31], TM[3][0:31, c * 128:(c + 1) * 128], IDENT[0:31, 0:31]),
                     reads=[TMb[3], IDb], writes=[pb], inc=(c == 7))
            ccw = col["cw"]
            K.op("dve", lambda h, p=p: h.tensor_copy(VT[:, ccw:ccw + 248], p[:, 0:248]), reads=[pb], writes=[VTb])
            w1d = conv_w_pw1[j].rearrange("(k p) n -> p k n", p=128)
            wv = None
            for c in range(8):
                if c % 2 == 0:
                    w, wb = wslot()
                    wv = w[:, 0:4096].rearrange("p (k a n) -> p k a n", k=8, a=2)
                    K.dma("pool", wv[:, :, 0, :], w1d[:, :, c * 128:(c + 2) * 128], writes=[wb])
                    K.dma("pool", wv[:, :, 1, :], w1d[:, :, D + c * 128:D + (c + 2) * 128], writes=[wb])
                o = (c % 2) * 128
                for hf in range(2):
                    sl = slice(hf * 512, (hf + 1) * 512)
                    pa, pab = ps()
                    pg, pgb = ps()
                    for a, (p, pb) in enumerate(((pa, pab), (pg, pgb))):
                        for k in range(8):
                            K.op("pe", lambda h, k=k, p=p, a=a, o=o, wv=wv, sl=sl: h.matmul(p[:], wv[:, k, a, o:o + 128], HT[:, k, sl], start=(k == 0), stop=(k == 7)),
                                 reads=[wb, HTb[k][hf]], writes=[pb], inc=(k == 7))
                    ta, tab = TM[hf * 2], TMb[hf * 2]
                    tg, tgb = TM[hf * 2 + 1], TMb[hf * 2 + 1]
                    K.op("act", lambda h, pa=pa, ta=ta, c=c: h.activation(out=ta[:, 0:512], in_=pa[:], func=AF.Identity, bias=vt("cv", c)), reads=[pab, VTb], writes=[tab])
                    K.op("act", lambda h, pg=pg, tg=tg, c=c: h.activation(out=tg[:, 0:512], in_=pg[:], func=AF.Sigmoid, bias=vt("cv", 8 + c)), reads=[pgb, VTb], writes=[tgb])
                    K.op("dve", lambda h, ta=ta, tg=tg, c=c, hf=hf: h.tensor_tensor(out=UP[c][:, 2 * hf:2 * hf + 2, 15:271], in0=ta[:, 0:512].rearrange("p (s t) -> p s t", s=2),
                                                                                 in1=tg[:, 0:512].rearrange("p (s t) -> p s t", s=2), op=ALU.mult),
                         reads=[tab, tgb], writes=[UPb[c]])
                K.op("dve", lambda h, c=c: h.tensor_scalar(out=UP[c][:, 1:4, 0:15], in0=UP[c][:, 0:3, 256:271], scalar1=FLAG[:, 0:1], scalar2=None, op0=ALU.mult), reads=[UPb[c], FLb], writes=[UPb[c]])
                K.op("dve", lambda h, c=c: h.tensor_scalar(out=UP[c][:, 0:3, 271:286], in0=UP[c][:, 1:4, 15:30], scalar1=FLAG[:, 0:1], scalar2=None, op0=ALU.mult), reads=[UPb[c], FLb], writes=[UPb[c]])
            alias(WKb, WKb)
            for c in range(8):
                dg, dgb = wslot()
                K.op("dve", lambda h, c=c, dg=dg: h.tensor_tensor(out=dg[:, 0:31 * 128].rearrange("p (k n) -> p k n", k=31), in0=IDENTB[:, :].unsqueeze(1).broadcast_to([128, 31, 128]),
                                                              in1=vt("cw", c * 31, 31).unsqueeze(2).broadcast_to([128, 31, 128]), op=ALU.mult), reads=[IDBb, VTb], writes=[dgb])
                for hf in range(2):
                    p, pb = ps()
                    for k in range(31):
                        K.op("pe", lambda h, k=k, c=c, p=p, dg=dg, hf=hf: h.matmul(p[:].rearrange("p (s t) -> p s t", s=2), dg[:, k * 128:(k + 1) * 128], UP[c][:, 2 * hf:2 * hf + 2, k:k + 256], start=(k == 0), stop=(k == 30)),
                             reads=[dgb, UPb[c]], writes=[pb], inc=(k == 30))
                    K.op("act", lambda h, p=p, c=c, hf=hf: h.activation(out=WK[c][:, hf * 512:(hf + 1) * 512], in_=p[:], func=AF.Identity, bias=vt("cv", 16 + c)), reads=[pb, VTb], writes=[WKb[c]])
            for hf in range(2):
                sl = slice(hf * 512, (hf + 1) * 512)
                pm_, pmb_ = ps()
                pq, pqb = ps()
                for c in range(8):
                    qi = sq_next[0]; sq_next[0] = (qi + 1) % 4
                    K.op("act", lambda h, c=c, qi=qi: h.copy(SQ[qi][:], WK[c][:, sl]), reads=[WKb[c]], writes=[SQb[qi]])
                    K.op("pe", lambda h, c=c, qi=qi, p=pm_: h.matmul(p[:], ONESB[:], SQ[qi][:], start=(c == 0), stop=(c == 7)), reads=[ONb, SQb[qi]], writes=[pmb_])
                    qi = sq_next[0]; sq_next[0] = (qi + 1) % 4
                    K.op("act", lambda h, c=c, qi=qi: h.activation(out=SQ[qi][:], in_=WK[c][:, sl], func=AF.Square), reads=[WKb[c]], writes=[SQb[qi]])
                    K.op("pe", lambda h, c=c, qi=qi, p=pq: h.matmul(p[:], ONESB[:], SQ[qi][:], start=(c == 0), stop=(c == 7)), reads=[ONb, SQb[qi]], writes=[pqb])
                mu, mub = TM[0], TMb[0]
                var, varb = TM[1], TMb[1]
                K.op("act", lambda h: h.copy(mu[:, 0:512], pm_[:]), reads=[pmb_], writes=[mub])
                K.op("dve", lambda h: h.tensor_tensor(out=var[:, 0:512], in0=mu[:, 0:512], in1=mu[:, 0:512], op=ALU.mult), reads=[mub], writes=[varb])
                K.op("dve", lambda h: h.tensor_tensor(out=var[:, 0:512], in0=pq[:], in1=var[:, 0:512], op=ALU.subtract), reads=[pqb, varb], writes=[varb])
                K.op("act", lambda h: h.activation(out=RSTD[:, sl], in_=var[:, 0:512], func=AF.Ln, bias=EPSV[:, 0:1]), reads=[varb, EPb], writes=[RSb[hf]])
                K.op("act", lambda h: h.activation(out=RSTD[:, sl], in_=RSTD[:, sl], func=AF.Exp, scale=-0.5), reads=[RSb[hf]], writes=[RSb[hf]])
                for c in range(8):
                    t2, t2b = TM[2 + (c % 2)], TMb[2 + (c % 2)]
                    K.op("dve", lambda h, c=c, t2=t2: h.tensor_tensor(out=t2[:, 0:512], in0=WK[c][:, sl], in1=mu[:, 0:512], op=ALU.subtract), reads=[WKb[c], mub], writes=[t2b])
                    K.op("dve", lambda h, c=c, t2=t2: h.tensor_tensor(out=t2[:, 0:512], in0=t2[:, 0:512], in1=RSTD[:, sl], op=ALU.mult), reads=[t2b, RSb[hf]], writes=[t2b])
                    K.op("act", lambda h, c=c, t2=t2: h.activation(out=HT[:, c, sl], in_=t2[:, 0:512], func=AF.Silu, scale=vt("cv", 24 + c), bias=vt("cv", 32 + c)),
                         reads=[t2b, VTb], writes=[HTb[c][hf]])
            out_proj(conv_w_pw2[j], 8, lambda k, sl: HT[:, k, sl], lambda k, hf: [HTb[k][hf]], "cv")
            alias(MTb, UPb + TMb)


        ACON = sb("ACON", (128, 64 + 128 + 1 + 2), F32)
        ACb = IDb
        K.dma("sp", ACON[:, 0:64], cmask[:, :], writes=[ACb])
        K.dma("sp", ACON[:, 192:193], cmk[:, :], writes=[ACb])
        J2B = sb("J2B", (128, 128), BF16)
        BLKB = sb("BLKB", (128, 128), BF16)
        ONE1B = sb("ONE1B", (128, 128), BF16)
        ABb = K.buf("ACONB")
        K.dma("pool", J2B[:], j2d[:, :], writes=[ABb])
        K.dma("pool", BLKB[:], blkd[:, :], writes=[ABb])
        K.op("dve", lambda h: h.memset(ONE1B[:], 1.0), writes=[ABb])

        def attn_mixer(i, j):
            norm_mod("A1", 0)
            r = lambda ap: ap.rearrange("(c p) -> c p", p=128)
            K.op("dve", lambda h: h.memset(vt("g1b", 0, 8), 0.0), writes=[VTb])
            for half in range(2):
                K.dma("sp", ACON[half * 64:(half + 1) * 64, 193:194], attn_q_norm[j].rearrange("(p o) -> p o", o=1), writes=[ACb])
                K.dma("sp", ACON[half * 64:(half + 1) * 64, 194:195], attn_k_norm[j].rearrange("(p o) -> p o", o=1), writes=[ACb])
            K.op("dve", lambda h: h.tensor_scalar(out=ACON[:, 193:194], in0=ACON[:, 193:194], scalar1=0.125, scalar2=None, op0=ALU.mult), reads=[ACb], writes=[ACb])
            names = ["VTOK", "OT", "QTE", "QTO", "KE", "KO", "KCE", "KCO", "CVP"]
            sizes = [8192, 8192, 1024, 1024, 1024, 1024, 512, 512, 512]
            reg_E = PAIR2[:, 5632:5632 + 960]
            reg = {}
            o = 0
            for nm, sz in zip(names, sizes):
                reg[nm] = MTflat[:, o:o + sz]
                o += sz
            assert o <= NPAIR * NT
            VTOK = reg["VTOK"].rearrange("p (t n) -> p t n", t=8)
            OT = reg["OT"].rearrange("p (c n) -> p c n", c=8)
            QTE, QTO, KE, KO, KCE, KCO = reg["QTE"], reg["QTO"], reg["KE"], reg["KO"], reg["KCE"], reg["KCO"]
            CVP = reg["CVP"].rearrange("p (t n) -> p t n", t=4)
            E = reg_E.rearrange("p (m q) -> p m q", m=15)
            VTOKb = [K.buf("VTOK%d" % t) for t in range(8)]
            OTb = [[K.buf("OT%d_%d" % (c, h)) for h in range(2)] for c in range(8)]
            QTEb, QTOb, KEb, KOb, KCEb, KCOb, CVPb, Eb = [K.buf(n) for n in ("QTE", "QTO", "KE", "KO", "KCE", "KCO", "CVP", "E")]
            allm = VTOKb + [b for l in OTb for b in l] + [QTEb, QTOb, KEb, KOb, KCEb, KCOb, CVPb]
            alias(allm, MTb)
            RP = WK[0][:, 0:960].rearrange("p (m q) -> p m q", m=15)
            EBT = [WK[1 + par][:, :].bitcast(BF16)[:, 0:1984].rearrange("p (m q) -> p m q", m=31) for par in range(2)]
            PTt = [WK[3][:, :].bitcast(BF16)[:, t * 512:(t + 1) * 512] for t in range(4)]
            Tt, Rr = WK[4][:, 0:512], WK[4][:, 512:1024]
            TN, KST = WK[5][:, 0:512], WK[5][:, 512:1024]
            CKP, RD = WK[6][:, 0:512].rearrange("p (t n) -> p t n", t=4), WK[6][:, 512:1024]
            VST = [WK[7][:, 0:512], WK[7][:, 512:1024]]
            RPb, PTb, Tb, Rb, TNb, KSTb, CKPb, RDb = K.buf("RP"), [K.buf("PT%d" % t) for t in range(4)], K.buf("T"), K.buf("R"), K.buf("TN"), K.buf("KST"), K.buf("CKP"), K.buf("RD")
            EBTb = [K.buf("EBT0"), K.buf("EBT1")]
            VSTb = [K.buf("VST0"), K.buf("VST1")]
            allw = [RPb, Tb, Rb, TNb, KSTb, CKPb, RDb] + PTb + EBTb + VSTb
            alias(allw, WKb)
            wq = attn_w_qkv[j].rearrange("(k p) n -> p k n", p=128)
            for ch in range(2):
                w, wb = wslot()
                wv = w[:, 0:4096].rearrange("p (k n) -> p k n", k=8)
                K.dma("pool", wv, wq[:, :, 2 * D + ch * 512:2 * D + (ch + 1) * 512], writes=[wb])
                for tt in range(8):
                    p, pb = ps()
                    for k in range(8):
                        K.op("pe", lambda h, k=k, p=p, wv=wv, tt=tt: h.matmul(p[:], HT[:, k, tt * 128:(tt + 1) * 128], wv[:, k, :], start=(k == 0), stop=(k == 7)),
                             reads=[wb, HTb[k][tt // 4]], writes=[pb], inc=(k == 7))
                    vs, vsb = VST[tt % 2], VSTb[tt % 2]
                    K.op("act", lambda h, p=p, vs=vs: h.copy(vs, p[:]), reads=[pb], writes=[vsb])
                    K.op("dve", lambda h, vs=vs, tt=tt, ch=ch: h.tensor_copy(VTOK[:, tt, ch * 512:(ch + 1) * 512], vs), reads=[vsb], writes=[VTOKb[tt]])
                    K.dma("sp", nv_d[j, tt * 128:(tt + 1) * 128, ch * 512:(ch + 1) * 512], vs, reads=[vsb], final=True)
            BS = []
            o2 = 0
            for si in range(2):
                d = {}
                if si == 0:
                    d.update(QTE=QTE, QTO=QTO, KE=KE, KO=KO, KCE=KCE, KCO=KCO, CVP=CVP, EBT=EBT,
                             QTEb=QTEb, QTOb=QTOb, KEb=KEb, KOb=KOb, KCEb=KCEb, KCOb=KCOb, CVPb=CVPb, EBTb=EBTb)
                else:
                    def take(n):
                        nonlocal_o = take.o
                        take.o += n
                        return PAIR2[:, nonlocal_o:nonlocal_o + n]
                    take.o = 0
                    d.update(QTE=take(1024), QTO=take(1024), KE=take(1024), KO=take(1024), KCE=take(512), KCO=take(512))
                    d["CVP"] = take(512).rearrange("p (t n) -> p t n", t=4)
                    d["EBT"] = [EBT2[:, par, :].rearrange("p (m q) -> p m q", m=31) for par in range(2)]
                    for nm in ("QTE", "QTO", "KE", "KO", "KCE", "KCO", "CVP"):
                        d[nm + "b"] = K.buf(nm + "_2")
                    d["EBTb"] = [K.buf("EBT0_2"), K.buf("EBT1_2")]
                BS.append(d)
            del PAIR2_bufs[:]
            PAIR2_bufs.extend([BS[1][n_ + "b"] for n_ in ("QTE", "QTO", "KE", "KO", "KCE", "KCO", "CVP")])
            for d in BS:
                K.op("pool", lambda h, d=d: h.memset(d["KE"], 0.0), writes=[d["KEb"]])
                K.op("pool", lambda h, d=d: h.memset(d["KO"], 0.0), writes=[d["KOb"]])
                K.dma("sp", d["KE"][64:80, :], rowsel_d[:, :], writes=[d["KEb"]])
                K.dma("sp", d["KO"][0:16, :], rowsel_d[:, :], writes=[d["KOb"]])
                K.op("pool", lambda h, d=d: h.memset(d["KCE"], 0.0), writes=[d["KCEb"]])
                K.op("pool", lambda h, d=d: h.memset(d["KCO"], 0.0), writes=[d["KCOb"]])
                for par in range(2):
                    K.op("pool", lambda h, d=d, par=par: h.memset(d["EBT"][par], 0.0), writes=[d["EBTb"][par]])
            Tq, Tk = WK[4], WK[5]
            Rq, Rk = WK[7], RSTD
            KSTs = [WK[4][:, 0:512], WK[4][:, 512:1024]]
            Tqb, Tkb, Rqb = K.buf("Tq"), K.buf("Tk"), K.buf("Rq")
            alias([Tqb, Tkb, Rqb], [Tb, Rb, TNb, KSTb] + VSTb)
            Rkb = K.buf("Rk")
            alias([Rkb], RSb)
            ps_banks[0] = [0, 1, 2, 7]
            wstate = {}

            def prep_stages(c):
                d = BS[c % 2]
                chains = ((Tq, Tqb, Rq, Rqb), (Tk, Tkb, Rk, Rkb))
                pj, sqs = {}, {}
                oc = (c % 2) * 128

                steps = []

                def st0a():
                    if c % 2 == 0:
                        w, wqb = wslot()
                        wqk = w[:, 0:4096].rearrange("p (k a n) -> p k a n", k=8, a=2)
                        K.dma("pool", wqk[:, :, 0, :], wq[:, :, c * 128:(c + 2) * 128], writes=[wqb])
                        K.dma("pool", wqk[:, :, 1, :], wq[:, :, D + c * 128:D + (c + 2) * 128], writes=[wqb])
                        wstate["w"] = (wqk, wqb)
                    K.dma("sp", CKP, ck_d[j, :, c * 128:(c + 1) * 128].rearrange("(t p) f -> p t f", p=128), writes=[CKPb])
                    K.dma("pool", d["CVP"], cv_d[j, :, c * 128:(c + 1) * 128].rearrange("(t p) f -> p t f", p=128), writes=[d["CVPb"]])
                steps.append(st0a)

                def mk_proj(a, hf):
                    def f():
                        wqk, wqb = wstate["w"]
                        T, Tb_, R, Rb_ = chains[a]
                        sl = slice(hf * 512, (hf + 1) * 512)
                        p, pb = ps()
                        for k in range(8):
                            K.op("pe", lambda h, k=k: h.matmul(p[:], wqk[:, k, a, oc:oc + 128], HT[:, k, sl], start=(k == 0), stop=(k == 7)),
                                 reads=[wqb, HTb[k][hf]], writes=[pb], inc=(k == 7))
                        K.op("dve", lambda h: h.tensor_copy(T[:, sl], p[:]), reads=[pb], writes=[Tb_])
                    return f

                def mk_sq(a, hf):
                    def f():
                        T, Tb_, R, Rb_ = chains[a]
                        sl = slice(hf * 512, (hf + 1) * 512)
                        qi = sq_next[0]; sq_next[0] = (qi + 1) % 4
                        sqs[(a, hf)] = qi
                        K.op("act", lambda h: h.activation(out=SQ[qi][:], in_=T[:, sl], func=AF.Square), reads=[Tb_], writes=[SQb[qi]])
                    return f

                def mk_stat(a, hf):
                    def f():
                        T, Tb_, R, Rb_ = chains[a]
                        sl = slice(hf * 512, (hf + 1) * 512)
                        qi = sqs[(a, hf)]
                        p2, p2b = ps()
                        K.op("pe", lambda h: h.matmul(p2[:], BLKB[:], SQ[qi][:], start=True, stop=True), reads=[ABb, SQb[qi]], writes=[p2b])
                        K.op("act", lambda h: h.activation(out=R[:, sl], in_=p2[:], func=AF.Ln, bias=EPSV[:, 0:1]), reads=[p2b, EPb], writes=[Rb_])
                    return f

                def mk_rexp(a):
                    def f():
                        T, Tb_, R, Rb_ = chains[a]
                        K.op("act", lambda h: h.activation(out=R[:], in_=R[:], func=AF.Exp, scale=-0.5), reads=[Rb_], writes=[Rb_])
                    return f

                for a in range(2):
                    for hf in range(2):
                        steps.append(mk_proj(a, hf))
                for a in range(2):
                    for hf in range(2):
                        steps.append(mk_sq(a, hf))
                for a in range(2):
                    for hf in range(2):
                        steps.append(mk_stat(a, hf))
                steps.append(mk_rexp(0))
                steps.append(mk_rexp(1))
                steps.append(lambda: K.op("dve", lambda h: h.scalar_tensor_tensor(out=d["QTE"], in0=Tq[:], scalar=ACON[:, 193:194], in1=Rq[:], op0=ALU.mult, op1=ALU.mult), reads=[Tqb, Rqb, ACb], writes=[d["QTEb"]]))
                steps.append(lambda: K.op("dve", lambda h: h.scalar_tensor_tensor(out=Tk[:], in0=Tk[:], scalar=ACON[:, 194:195], in1=Rk[:], op0=ALU.mult, op1=ALU.mult), reads=[Tkb, Rkb, ACb], writes=[Tkb]))
                steps.append(lambda: K.op("act", lambda h: h.copy(d["QTO"], d["QTE"]), reads=[d["QTEb"]], writes=[d["QTOb"]]))
                steps.append(lambda: K.op("act", lambda h: h.copy(d["KE"][0:64, :], Tk[0:64, :]), reads=[Tkb], writes=[d["KEb"]]))
                steps.append(lambda: K.op("act", lambda h: h.copy(d["KO"][64:128, :], Tk[64:128, :]), reads=[Tkb], writes=[d["KOb"]]))

                def st3a():
                    K.dma("sp", d["QTE"][64:80, :], pmadd_d[:, :], reads=[d["QTOb"]], writes=[d["QTEb"]])
                    K.dma("sp", d["QTO"][0:16, :], pmadd_d[:, :], writes=[d["QTOb"]])
                steps.append(st3a)

                def mk_kt(hf):
                    def f():
                        p3, p3b = ps()
                        for t4 in range(4):
                            K.op("pe", lambda h, t4=t4: h.transpose(p3[:, t4 * 128:(t4 + 1) * 128], Tk[:, hf * 512 + t4 * 128:hf * 512 + (t4 + 1) * 128], IDENT[:]), reads=[Tkb, IDb], writes=[p3b], inc=(t4 == 3))
                        K.op("dve", lambda h: h.tensor_copy(KSTs[hf], p3[:]), reads=[p3b], writes=[Tqb])
                        K.dma("sp", nk_d[j, hf * 512:(hf + 1) * 512, c * 128:(c + 1) * 128].rearrange("(t p) f -> p t f", p=128), KSTs[hf].rearrange("p (t f) -> p t f", t=4), reads=[Tqb], final=True)
                    return f
                steps.append(mk_kt(0))
                steps.append(mk_kt(1))

                def st3c():
                    p4, p4b = ps()
                    for t4 in range(4):
                        K.op("pe", lambda h, t4=t4: h.transpose(p4[:, t4 * 128:(t4 + 1) * 128], CKP[:, t4, :], IDENT[:]), reads=[CKPb, IDb], writes=[p4b], inc=(t4 == 3))
                    K.op("dve", lambda h: h.tensor_copy(d["KCE"][0:64, :], p4[0:64, :]), reads=[p4b], writes=[d["KCEb"]])
                    K.op("dve", lambda h: h.tensor_copy(d["KCO"][64:128, :], p4[64:128, :]), reads=[p4b], writes=[d["KCOb"]])
                steps.append(st3c)

                def eb_dma(par):
                    def f():
                        hd = 2 * c + par
                        src = bass.AP(tensor=rpbr_t, offset=(j * 16 + hd) * 15 * 127, ap=[[1, 64], [127, 15], [1, 64]])
                        K.dma("sp", RP[0:64, :, :], src, writes=[RPb])
                        K.dma("sp", RP[64:128, :, :], src, writes=[RPb])
                    return f

                def eb(par):
                    def f():
                        K.op("act", lambda h: h.activation(out=E, in_=RP, func=AF.Exp), reads=[RPb], writes=[Eb])
                        K.op("dve", lambda h: h.tensor_tensor(out=E, in0=E, in1=ACON[:, 0:64].unsqueeze(1).broadcast_to([128, 15, 64]), op=ALU.mult), reads=[Eb, ACb], writes=[Eb])
                    return f

                def eb2(par, g):
                    def f():
                        Ef = reg_E
                        m0, nm = ((0, 8), (8, 7))[g]
                        p5, p5b = ps()
                        K.op("pe", lambda h: h.matmul(p5[:, 0:nm * 64], J2B[:], Ef[:, m0 * 64:(m0 + nm) * 64], start=True, stop=True), reads=[ABb, Eb], writes=[p5b])
                        K.op("dve", lambda h: h.tensor_copy(d["EBT"][par][0:64, 8 + m0:8 + m0 + nm, :], p5[0:64, 0:nm * 64].rearrange("p (m q) -> p m q", m=nm)), reads=[p5b], writes=[d["EBTb"][par]])
                        K.op("dve", lambda h: h.tensor_copy(d["EBT"][par][64:128, 9 + m0:9 + m0 + nm, :], p5[64:128, 0:nm * 64].rearrange("p (m q) -> p m q", m=nm)), reads=[p5b], writes=[d["EBTb"][par]])
                    return f
                steps.insert(1, eb_dma(0))
                k_mid = 1 + len(steps) // 2
                tail = steps[k_mid:]
                steps = steps[:k_mid] + [eb(0), eb_dma(1), eb2(0, 0), eb2(0, 1)] + tail + [eb(1), eb2(1, 0), eb2(1, 1)]
                return steps


            def attention(c, nxt):
                d = BS[c % 2]
                jobs = []
                for hf in range(2):
                    for par in range(2):
                        tl = [("ctx", t) for t in range(4)] + [("loc", r2) for r2 in (range(0, 6) if hf == 0 else range(2, 8))]
                        for idx, (kind_, t) in enumerate(tl):
                            jobs.append((hf, par, idx, kind_, t, idx == len(tl) - 1))
                LAG = 2
                state = {}

                def stageAB(n):
                    hf, par, idx, kind_, t, last = jobs[n]
                    sl = slice(hf * 512, (hf + 1) * 512)
                    Kx, Kxb = (d["KE"], d["KEb"]) if par == 0 else (d["KO"], d["KOb"])
                    KCx, KCxb = (d["KCE"], d["KCEb"]) if par == 0 else (d["KCO"], d["KCOb"])
                    Qx, Qxb = (d["QTE"], d["QTEb"]) if par == 0 else (d["QTO"], d["QTOb"])
                    sp_, spb = ps()
                    pi = n % 4
                    pt, ptb = PTt[pi], PTb[pi]
                    if kind_ == "ctx":
                        K.op("pe", lambda h: h.matmul(sp_[:], KCx[:, t * 128:(t + 1) * 128], Qx[:, sl], start=True, stop=True), reads=[KCxb, Qxb], writes=[spb])
                        K.op("act", lambda h: h.activation(out=pt, in_=sp_[:], func=AF.Exp, bias=ACON[:, 192:193]), reads=[spb, ACb], writes=[ptb])
                    else:
                        K.op("pe", lambda h: h.matmul(sp_[:], Kx[:, t * 128:(t + 1) * 128], Qx[:, sl], start=True, stop=True), reads=[Kxb, Qxb], writes=[spb])
                        K.op("act", lambda h: h.activation(out=pt, in_=sp_[:], func=AF.Exp), reads=[spb], writes=[ptb])
                        m0 = 15 - 2 * t + 8 * hf
                        pt3 = pt.rearrange("p (m q) -> p m q", m=8)
                        K.op("dve", lambda h: h.tensor_tensor(out=pt3, in0=pt3, in1=d["EBT"][par][:, m0:m0 + 8, :], op=ALU.mult), reads=[ptb, d["EBTb"][par]], writes=[ptb])
                    state[n] = (pt, ptb)

                def stageC(n):
                    hf, par, idx, kind_, t, last = jobs[n]
                    sl = slice(hf * 512, (hf + 1) * 512)
                    pt, ptb = state.pop(n)
                    Ops, Opb = PS[3 + par], PSb[3 + par]
                    Dps, Dpb = PS[5 + par], PSb[5 + par]
                    if kind_ == "ctx":
                        K.op("pe", lambda h: h.matmul(Ops[:], d["CVP"][:, t, :], pt, start=(idx == 0), stop=False), reads=[d["CVPb"], ptb], writes=[Opb], inc=False)
                    else:
                        K.op("pe", lambda h: h.matmul(Ops[:], VTOK[:, t, c * 128:(c + 1) * 128], pt, start=(idx == 0), stop=last), reads=[VTOKb[t], ptb], writes=[Opb], inc=False)
                    K.op("pe", lambda h: h.matmul(Dps[:], ONE1B[:], pt, start=(idx == 0), stop=last), reads=[ABb, ptb], writes=[Dpb, Opb], inc=True)
                    if last:
                        rows = slice(par * 64, par * 64 + 64)
                        K.op("act", lambda h: h.activation(out=RD[rows, :], in_=Dps[rows, :], func=AF.Ln), reads=[Dpb], writes=[RDb])
                        K.op("act", lambda h: h.activation(out=RD[rows, :], in_=RD[rows, :], func=AF.Exp, scale=-1.0), reads=[RDb], writes=[RDb])
                        K.op("dve", lambda h: h.tensor_tensor(out=OT[rows, c, sl], in0=Ops[rows, :], in1=RD[rows, :], op=ALU.mult), reads=[Opb, RDb], writes=[OTb[c][hf]])

                sched = {}
                for k_, st in enumerate(nxt):
                    sched.setdefault(int(k_ * 37 / max(1, len(nxt))), []).append(st)
                for n in range(len(jobs) + LAG):
                    for st in sched.get(n, ()):
                        st()
                    if n < len(jobs):
                        stageAB(n)
                    if n - LAG >= 0:
                        stageC(n - LAG)

            for st in prep_stages(0):
                st()
            for c in range(8):
                if astage < 3:
                    if c + 1 < 8:
                        for st in prep_stages(c + 1):
                            st()
                    continue
                if cfg.get("interleave", 1):
                    attention(c, prep_stages(c + 1) if c + 1 < 8 else [])
                else:
                    attention(c, [])
                    if c + 1 < 8:
                        for st in prep_stages(c + 1):
                            st()
            ps_banks[0] = list(range(7))
            if astage < 3:
                K.op("pool", lambda h: h.memset(reg["OT"], 0.0), writes=[b for l in OTb for b in l])
            out_proj(attn_w_o[j], 8, lambda k, sl: OT[:, k, sl], lambda k, hf: [OTb[k][hf]], None)
            alias(MTb, allm)
            alias(WKb, allw + [Tqb, Tkb, Rqb])
            alias(RSb, [Rkb])


        GMASK = sb("GMASK", (128, 256), F32)
        GMb = IDb
        K.dma("sp", GMASK[:, 0:128], maskf_d[:, :], writes=[GMb])
        K.dma("sp", GMASK[:, 128:256], maskb_d[:, :], writes=[GMb])
        ONEV = sb("ONEV", (128, 1), F32)
        OVb = K.buf("ONEV")
        K.op("dve", lambda h: h.memset(ONEV[:], 1.0), writes=[OVb])
        ONES256 = sb("ONES256", (128, 128), BF16)
        O256b = K.buf("ONES256")
        K.op("dve", lambda h: h.memset(ONES256[:], 1.0 / 256), writes=[O256b])

        def gla_mixer(i, j):
            norm_mod("A1", 0)
            K.op("dve", lambda h: h.memset(vt("g1b", 0, 8), 0.0), writes=[VTb])
            vec_rows(0, 0, gla_b_gk[j].rearrange("d (c p) -> (d c) p", p=128), 8)
            vec_rows(0, 8, gla_o_norm[j].rearrange("(c p) -> c p", p=128), 2)
            stg_to_vt(0, 10, ["gv"])
            K.op("dve", lambda h: h.tensor_scalar(out=vt("gv", 0, 8), in0=vt("gv", 0, 8), scalar1=-1.0, scalar2=None, op0=ALU.mult), reads=[VTb], writes=[VTb])
            names = ["VTOK", "OT", "QG", "KG", "KOTK", "ATT", "SBF", "AT", "W2P", "W1C"]
            sizes = [8192, 8192, 1024, 1024, 512, 512, 256, 1024, 1024, 256]
            reg = {}
            o = 0
            for nm, sz in zip(names, sizes):
                reg[nm] = MTflat[:, o:o + sz]
                o += sz
            assert o <= NPAIR * NT
            VTOK = reg["VTOK"].rearrange("p (t n) -> p t n", t=8)
            OT = reg["OT"].rearrange("p (c n) -> p c n", c=8)
            QG, KG, SBF, AT = reg["QG"], reg["KG"], reg["SBF"], reg["AT"]
            KOTK = [reg["KOTK"][:, a * 128:(a + 1) * 128] for a in range(4)]
            ATT = [reg["ATT"][:, a * 128:(a + 1) * 128] for a in range(4)]
            W2P = reg["W2P"].rearrange("p (d n) -> p d n", d=2)
            W1C = reg["W1C"].rearrange("p (k r) -> p k r", k=8)
            VTOKb = [K.buf("gVTOK%d" % t) for t in range(8)]
            OTb = [[K.buf("gOT%d_%d" % (c, h)) for h in range(2)] for c in range(8)]
            QGb, KGb, SBFb, ATb, W2Pb, W1Cb = [K.buf(n) for n in ("QG", "KG", "SBF", "AT", "W2P", "W1C")]
            KOTKb = [K.buf("KOTK%d" % a) for a in range(4)]
            ATTb = [K.buf("ATT%d" % a) for a in range(4)]
            allm = VTOKb + [b for l in OTb for b in l] + [QGb, KGb, SBFb, ATb, W2Pb, W1Cb] + KOTKb + ATTb
            alias(allm, MTb)
            QF, KF, GL, PM, TX, KOT = WK[0], WK[1], WK[2], WK[3], WK[4], WK[5]
            OF = [WK[6], WK[7]]
            QFb, KFb, GLb, PMb, TXb, KOTb = [K.buf(n) for n in ("QF", "KF", "GL", "PM", "TX", "KOT")]
            OFb = [[K.buf("OF%d_%d" % (vc, c)) for c in range(8)] for vc in range(2)]
            allw = [QFb, KFb, GLb, PMb, TXb, KOTb] + [b for l in OFb for b in l]
            alias(allw, WKb)
            Sst = RSTD[:, 512:768]
            DEC = RSTD[:, 768:776]
            RES = PAIR2[:, 0:NT]
            RESb, Sb, DECb = K.buf("RESET"), K.buf("S"), K.buf("DEC")
            alias([RESb], PAIR2_bufs)
            alias([Sb, DECb], [RSb[1]])
            K.op("pool", lambda h: h.memset(RES, 1.0), writes=[RESb])
            K.op("pool", lambda h: h.memset(RES.rearrange("p (c t) -> p c t", t=128)[:, :, 0:1], 0.0), reads=[RESb], writes=[RESb])
            K.op("pool", lambda h: h.memset(reg["W2P"], 0.0), writes=[W2Pb])
            for d in range(2):
                K.dma("pool", W1C[:, :, d * 16:(d + 1) * 16], gla_w_gk1[j, d].rearrange("(k p) r -> p k r", p=128), writes=[W1Cb])
                K.dma("pool", W2P[d * 16:(d + 1) * 16, d, :], gla_w_gk2[j, d], writes=[W2Pb])
            for hf in range(2):
                sl = slice(hf * 512, (hf + 1) * 512)
                p, pb = ps()
                for k in range(8):
                    K.op("pe", lambda h, k=k, p=p, sl=sl: h.matmul(p[0:32, :], W1C[:, k, :], HT[:, k, sl], start=(k == 0), stop=(k == 7)), reads=[W1Cb, HTb[k][hf]], writes=[pb], inc=(k == 7))
                K.op("act", lambda h, p=p, sl=sl: h.copy(AT[0:32, sl], p[0:32, :]), reads=[pb], writes=[ATb])
            wvd = gla_w_v[j].rearrange("(k p) n -> p k n", p=128)
            for ch in range(2):
                w, wb = wslot()
                wv = w[:, 0:4096].rearrange("p (k n) -> p k n", k=8)
                K.dma("pool", wv, wvd[:, :, ch * 512:(ch + 1) * 512], writes=[wb])
                for tt in range(8):
                    p, pb = ps()
                    for k in range(8):
                        K.op("pe", lambda h, k=k, p=p, wv=wv, tt=tt: h.matmul(p[:], HT[:, k, tt * 128:(tt + 1) * 128], wv[:, k, :], start=(k == 0), stop=(k == 7)),
                             reads=[wb, HTb[k][tt // 4]], writes=[pb], inc=(k == 7))
                    K.op("act", lambda h, p=p, tt=tt, ch=ch: h.copy(VTOK[:, tt, ch * 512:(ch + 1) * 512], p[:]), reads=[pb], writes=[VTOKb[tt]])
            gwq = gla_w_q[j].rearrange("(k p) n -> p k n", p=128)
            gwk = gla_w_k[j].rearrange("(k p) n -> p k n", p=128)
            wgd = gla_w_g[j].rearrange("(k p) n -> p k n", p=128)
            wgt, wgb = None, None
            QG2, KG2 = PAIR2[:, 1024:2048], PAIR2[:, 2048:3072]
            KOT2 = PAIR2[:, 3072:5120].bitcast(F32)
            DEC2 = RSTD[:, 776:784]
            QG2b, KG2b, KOT2b, DEC2b = K.buf("QG2"), K.buf("KG2"), K.buf("KOT2"), K.buf("DEC2")
            alias([QG2b, KG2b, KOT2b], PAIR2_bufs)
            alias([DEC2b], [DECb])
            SETS = [dict(QG=QG, QGb=QGb, KG=KG, KGb=KGb, KOT=KOT[:, :], KOTb=KOTb, DEC=DEC, DECb=DECb),
                    dict(QG=QG2, QGb=QG2b, KG=KG2, KGb=KG2b, KOT=KOT2, KOTb=KOT2b, DEC=DEC2, DECb=DEC2b)]
            wst = {}
            PM3 = PM[:, :].rearrange("p (c t) -> p c t", t=128)
            TX3 = TX[:, :].rearrange("p (c t) -> p c t", t=128)

            def setup_steps(hd, d):
                B = SETS[d]
                edge = 127 if d == 0 else 0
                steps = []
                if d == 0:
                    def ld():
                        w, wqb = wslot()
                        wqk = w[:, 0:2048].rearrange("p (k a n) -> p k a n", k=8, a=2)
                        K.dma("pool", wqk[:, :, 0, :], gwq[:, :, hd * 128:(hd + 1) * 128], writes=[wqb])
                        K.dma("pool", wqk[:, :, 1, :], gwk[:, :, hd * 128:(hd + 1) * 128], writes=[wqb])
                        wst["w"] = (wqk, wqb)
                    steps.append(ld)

                    def mk_qk(a, hf):
                        def f():
                            wqk, wqb = wst["w"]
                            dst, dstb, scl = ((QF, QFb, 128.0 ** -0.5), (KF, KFb, 1.0))[a]
                            sl = slice(hf * 512, (hf + 1) * 512)
                            p, pb = ps()
                            for k in range(8):
                                K.op("pe", lambda h, k=k: h.matmul(p[:], wqk[:, k, a, :], HT[:, k, sl], start=(k == 0), stop=(k == 7)),
                                     reads=[wqb, HTb[k][hf]], writes=[pb], inc=(k == 7))
                            K.op("act", lambda h: h.activation(out=dst[:, sl], in_=p[:], func=AF.Copy, scale=scl), reads=[pb], writes=[dstb])
                        return f
                    for a in range(2):
                        for hf in range(2):
                            steps.append(mk_qk(a, hf))

                def mk_gl(hf):
                    def f():
                        sl = slice(hf * 512, (hf + 1) * 512)
                        p, pb = ps()
                        K.op("pe", lambda h: h.matmul(p[:], W2P[0:32, d, hd * 128:(hd + 1) * 128], AT[0:32, sl], start=True, stop=True), reads=[W2Pb, ATb], writes=[pb])
                        K.op("act", lambda h: h.activation(out=TX[:, sl], in_=p[:], func=AF.Exp, scale=-1.0, bias=vt("gv", d * 4 + hd)), reads=[pb, VTb], writes=[TXb])
                        K.op("act", lambda h: h.activation(out=GL[:, sl], in_=TX[:, sl], func=AF.Ln, bias=ONEV[:, 0:1]), reads=[TXb, OVb], writes=[GLb])
                    return f
                steps.append(mk_gl(0))
                steps.append(mk_gl(1))
                steps.append(lambda: K.op("dve", lambda h: h.tensor_tensor_scan(out=PM[:], data0=RES, data1=GL[:], initial=0.0, op0=ALU.mult, op1=ALU.add), reads=[RESb, GLb], writes=[PMb]))
                if d == 1:
                    steps.append(lambda: K.op("dve", lambda h: h.tensor_tensor(out=TX3, in0=PM3[:, :, 127:128].broadcast_to([128, 8, 128]), in1=PM3, op=ALU.subtract), reads=[PMb], writes=[TXb]))
                    steps.append(lambda: K.op("dve", lambda h: h.tensor_tensor(out=PM[:], in0=TX[:], in1=GL[:], op=ALU.add), reads=[TXb, GLb], writes=[PMb]))
                steps.append(lambda: K.op("act", lambda h: h.activation(out=B["DEC"].rearrange("p (c o) -> p c o", o=1), in_=PM3[:, :, edge:edge + 1], func=AF.Exp, scale=-1.0 / 16), reads=[PMb], writes=[B["DECb"]]))
                steps.append(lambda: K.op("act", lambda h: h.activation(out=TX[:], in_=PM[:], func=AF.Exp, scale=-1.0 / 16), reads=[PMb], writes=[TXb]))
                steps.append(lambda: K.op("dve", lambda h: h.tensor_tensor(out=B["QG"], in0=QF[:], in1=TX[:], op=ALU.mult), reads=[QFb, TXb], writes=[B["QGb"]]))
                steps.append(lambda: K.op("act", lambda h: h.activation(out=TX[:], in_=PM[:], func=AF.Exp, scale=1.0 / 16), reads=[PMb], writes=[TXb]))
                steps.append(lambda: K.op("dve", lambda h: h.tensor_tensor(out=B["KG"], in0=KF[:], in1=TX[:], op=ALU.mult), reads=[KFb, TXb], writes=[B["KGb"]]))
                steps.append(lambda: K.op("dve", lambda h: h.tensor_tensor(out=TX3, in0=PM3, in1=PM3[:, :, edge:edge + 1].broadcast_to([128, 8, 128]), op=ALU.subtract), reads=[PMb], writes=[TXb]))
                steps.append(lambda: K.op("act", lambda h: h.activation(out=TX[:], in_=TX[:], func=AF.Exp, scale=1.0 / 16), reads=[TXb], writes=[TXb]))
                steps.append(lambda: K.op("dve", lambda h: h.tensor_tensor(out=B["KOT"], in0=KF[:], in1=TX[:], op=ALU.mult), reads=[KFb, TXb], writes=[B["KOTb"]]))
                return steps

            def chunks(hd, d, nxt):
                B = SETS[d]
                QGx, QGxb, KGx, KGxb, KOTx, KOTxb, DECx, DECxb = B["QG"], B["QGb"], B["KG"], B["KGb"], B["KOT"], B["KOTb"], B["DEC"], B["DECb"]
                K.dma("sp", Sst, (s0f_d if d == 0 else s0b_d)[hd], writes=[Sb])
                K.op("act", lambda h: h.copy(SBF, Sst), reads=[Sb], writes=[SBFb])
                order = list(range(8)) if d == 0 else list(range(7, -1, -1))

                def gA(ci):
                    c = order[ci]
                    cs = slice(c * 128, (c + 1) * 128)
                    ai = ci % 4
                    p, pb = ps()
                    K.op("pe", lambda h: h.matmul(p[:, 0:128], KGx[:, cs], QGx[:, cs], start=True, stop=True), reads=[KGxb, QGxb], writes=[pb])
                    K.op("dve", lambda h: h.tensor_tensor(out=ATT[ai], in0=p[:, 0:128], in1=GMASK[:, d * 128:(d + 1) * 128], op=ALU.mult), reads=[pb, GMb], writes=[ATTb[ai]])
                    p2, p2b = ps()
                    K.op("pe", lambda h: h.transpose(p2[:, 0:128], KOTx[:, cs], IDENT[:]), reads=[KOTxb, IDb], writes=[p2b])
                    K.op("act", lambda h: h.copy(KOTK[ai], p2[:, 0:128]), reads=[p2b], writes=[KOTKb[ai]])

                def gB(ci):
                    c = order[ci]
                    cs = slice(c * 128, (c + 1) * 128)
                    ai = ci % 4
                    p4, p4b = ps()
                    K.op("pe", lambda h: h.matmul(p4[:, 0:256], KOTK[ai], VTOK[:, c, hd * 256:(hd + 1) * 256], start=True, stop=True), reads=[KOTKb[ai], VTOKb[c]], writes=[p4b])
                    p3, p3b = ps()
                    for vc in range(2):
                        K.op("pe", lambda h, vc=vc: h.matmul(p3[:, vc * 128:(vc + 1) * 128], VTOK[:, c, hd * 256 + vc * 128:hd * 256 + (vc + 1) * 128], ATT[ai], start=True, stop=False),
                             reads=[VTOKb[c], ATTb[ai]], writes=[p3b], inc=False)
                        K.op("pe", lambda h, vc=vc: h.matmul(p3[:, vc * 128:(vc + 1) * 128], SBF[:, vc * 128:(vc + 1) * 128], QGx[:, cs], start=False, stop=True),
                             reads=[SBFb, QGxb], writes=[p3b], inc=(vc == 1))
                    K.op("dve", lambda h: h.scalar_tensor_tensor(out=Sst, in0=Sst, scalar=DECx[:, c:c + 1], in1=p4[:, 0:256], op0=ALU.mult, op1=ALU.add), reads=[Sb, DECxb, p4b], writes=[Sb])
                    seg_end = (c % 2 == 1) if d == 0 else (c % 2 == 0)
                    if seg_end:
                        K.dma("sp", (gsf_d if d == 0 else gsb_d)[c // 2, hd], Sst, reads=[Sb], final=True)
                        if ci < 7:
                            K.op("dve", lambda h: h.tensor_scalar(out=Sst, in0=Sst, scalar1=FLAG[:, 0:1], scalar2=None, op0=ALU.mult), reads=[Sb, FLb], writes=[Sb])
                    if ci < 7:
                        K.op("act", lambda h: h.copy(SBF, Sst), reads=[Sb], writes=[SBFb])
                    for vc in range(2):
                        if d == 0:
                            K.op("act", lambda h, vc=vc: h.copy(OF[vc][:, cs], p3[:, vc * 128:(vc + 1) * 128]), reads=[p3b], writes=[OFb[vc][c]])
                        else:
                            K.op("dve", lambda h, vc=vc: h.tensor_tensor(out=OF[vc][:, cs], in0=p3[:, vc * 128:(vc + 1) * 128], in1=OF[vc][:, cs], op=ALU.add), reads=[p3b, OFb[vc][c]], writes=[OFb[vc][c]])

                GLAG = 2
                nit = 8 + GLAG
                sched = {}
                for k_, st in enumerate(nxt):
                    sched.setdefault(min(nit - 1, int(k_ * nit / max(1, len(nxt)))), []).append(st)
                for ci in range(nit):
                    for st in sched.get(ci, ()):
                        st()
                    if ci < 8:
                        gA(ci)
                    if ci - GLAG >= 0:
                        gB(ci - GLAG)

            def onorm(hd):
                nonlocal_w = wst
                if hd % 2 == 0:
                    wgt, wgb = wslot()
                    wgv = wgt[:, 0:4096].rearrange("p (k n) -> p k n", k=8)
                    K.dma("pool", wgv, wgd[:, :, (hd // 2) * 512:(hd // 2 + 1) * 512], writes=[wgb])
                    wst["g"] = (wgv, wgb)
                wgv, wgb = wst["g"]
                for hf in range(2):
                    sl = slice(hf * 512, (hf + 1) * 512)
                    ofb = lambda vc: [OFb[vc][c] for c in range(hf * 4, hf * 4 + 4)]
                    p, pb = ps()
                    for vc in range(2):
                        qi = sq_next[0]; sq_next[0] = (qi + 1) % 4
                        K.op("act", lambda h, vc=vc, qi=qi: h.activation(out=SQ[qi][:], in_=OF[vc][:, sl], func=AF.Square), reads=ofb(vc), writes=[SQb[qi]])
                        K.op("pe", lambda h, p=p, vc=vc, qi=qi: h.matmul(p[:], ONES256[:], SQ[qi][:], start=(vc == 0), stop=(vc == 1)), reads=[O256b, SQb[qi]], writes=[pb])
                    K.op("act", lambda h, p=p: h.activation(out=RSTD[:, 0:512], in_=p[:], func=AF.Ln, bias=EPSV[:, 0:1]), reads=[pb, EPb], writes=[RSb[0]])
                    K.op("act", lambda h: h.activation(out=RSTD[:, 0:512], in_=RSTD[:, 0:512], func=AF.Exp, scale=-0.5), reads=[RSb[0]], writes=[RSb[0]])
                    for vc in range(2):
                        ch = hd * 2 + vc
                        oc = (ch % 4) * 128
                        pg, pgb = ps()
                        for k in range(8):
                            K.op("pe", lambda h, k=k, pg=pg, oc=oc, sl=sl: h.matmul(pg[:], wgv[:, k, oc:oc + 128], HT[:, k, sl], start=(k == 0), stop=(k == 7)), reads=[wgb, HTb[k][hf]], writes=[pg_b(pgb)], inc=(k == 7))
                        K.op("act", lambda h, pg=pg: h.activation(out=TX[:, 0:512], in_=pg[:], func=AF.Silu), reads=[pgb], writes=[TXb])
                        K.op("dve", lambda h, vc=vc: h.scalar_tensor_tensor(out=TX[:, 512:1024], in0=OF[vc][:, sl], scalar=vt("gv", 8 + vc), in1=RSTD[:, 0:512], op0=ALU.mult, op1=ALU.mult), reads=ofb(vc) + [VTb, RSb[0], TXb], writes=[TXb])
                        K.op("dve", lambda h, ch=ch: h.tensor_tensor(out=OT[:, ch, sl], in0=TX[:, 512:1024], in1=TX[:, 0:512], op=ALU.mult), reads=[TXb], writes=[OTb[ch][hf]])
            for st in setup_steps(0, 0):
                st()
            for u in range(8):
                hd, d = divmod(u, 2)
                nxt = setup_steps(*divmod(u + 1, 2)) if u + 1 < 8 else []
                chunks(hd, d, nxt)
                if d == 1:
                    onorm(hd)
            out_proj(gla_w_o[j], 8, lambda k, sl: OT[:, k, sl], lambda k, hf: [OTb[k][hf]], None)
            alias(MTb, allm)
            alias(WKb, allw)
            alias([RSb[1]], [Sb, DECb, DEC2b])

        def pg_b(b):
            return b

        mod_compute(0, (0,))
        for i in range(nlayers):
            layer_vectors(i)
            kind, j = i % 3, i // 3
            if kind in mixers:
                if kind == 1:
                    conv_mixer(i, j)
                if kind == 0:
                    attn_mixer(i, j)
                if kind == 2:
                    gla_mixer(i, j)
            ffn(i)

        K.finish()
    return nc


_CACHE = {}


def _run(inputs, cfg):
    key = repr(sorted(cfg.items()))
    if key not in _CACHE:
        _CACHE[key] = build_program(cfg)
    nc = _CACHE[key]
    f32 = lambda a: np.ascontiguousarray(np.asarray(a, dtype=np.float32))
    xp = f32(inputs["x_prompt"])
    xs = f32(inputs["x_sample"])
    c = f32(inputs["c"])
    cctx = f32(inputs["c_ctx"])
    shared = {n: f32(inputs[n]) for n in ("mod_w", "mod_b", "norm1_g", "norm2_g", "ffn_w_up", "ffn_b_up", "ffn_w_dw",
                                          "ffn_b_dw", "ffn_w_down", "ffn_b_down", "conv_w_pw1", "conv_b_pw1", "conv_w_dw", "conv_b_dw", "conv_ln_g", "conv_ln_b", "conv_w_pw2", "conv_b_pw2")}
    shared["ident"] = np.eye(128, dtype=np.float32)
    for n in ("attn_w_qkv", "attn_w_o", "attn_q_norm", "attn_k_norm", "gla_w_q", "gla_w_k", "gla_w_v", "gla_w_g", "gla_w_gk1", "gla_w_gk2", "gla_b_gk", "gla_o_norm", "gla_w_o"):
        shared[n] = f32(inputs[n])
    j64 = np.eye(64, dtype=np.float32)[::-1]
    j2 = np.zeros((128, 128), np.float32); j2[:64, :64] = j64; j2[64:, 64:] = j64
    blk = np.zeros((128, 128), np.float32); blk[:64, :64] = 1.0 / 64; blk[64:, 64:] = 1.0 / 64
    shared["j2"] = j2
    shared["blk"] = blk
    rpb = f32(inputs["attn_rpb"])
    rpbr_s = np.zeros((2, 16, 15, 127), np.float32)
    ys = np.arange(127)
    valid = (78 - ys >= 0) & (78 - ys <= 30)
    rpbr_s[:, :, :, valid] = rpb[:, :, ::-1, :][:, :, :, (78 - ys)[valid]]
    rpbr_p = np.zeros_like(rpbr_s)
    qc = np.arange(64)
    cstart = np.clip(qc - 8, 0, 48)
    kc_of_p = 63 - (np.arange(128) % 64)
    cm_s = ((kc_of_p[:, None] >= cstart[None, :]) & (kc_of_p[:, None] < cstart[None, :] + 16)).astype(np.float32)
    cm_p = np.ones((128, 64), np.float32)
    rho = 2 * np.arange(8)[None, :, None] + (np.arange(128) // 64)[:, None, None]
    jj = np.arange(16)[None, None, :]
    rs = np.clip(jj - 4, 0, 8)
    rr = np.arange(16)[:, None]
    jq = (np.arange(1024) // 64)[None, :]
    rsq = np.clip(jq - 4, 0, 8)
    pm_s = np.where((rr >= rsq) & (rr < rsq + 8), 0.0, -30000.0).astype(np.float32)
    pm_p = np.where((rr // 4) == (jq // 4), 0.0, -30000.0).astype(np.float32)
    import ml_dtypes
    shared["rowsel"] = (np.arange(1024)[None, :] // 64 == np.arange(16)[:, None]).astype(np.float32).astype(ml_dtypes.bfloat16)
    pm_s = pm_s.astype(ml_dtypes.bfloat16); pm_p = pm_p.astype(ml_dtypes.bfloat16)
    tri = np.tril(np.ones((128, 128), np.float32))
    shared["maskf"] = np.ascontiguousarray(tri.T)
    shared["maskb"] = np.ascontiguousarray(tri)
    sgf = f32(inputs["state_gla_fwd"])[:, 0]
    sgb = f32(inputs["state_gla_bwd"])[:, 0]
    zs = np.zeros((4, 128, 256), np.float32)
    ck = f32(inputs["cache_attn_k"]).reshape(4, 2, 512, D)
    cv = f32(inputs["cache_attn_v"]).reshape(4, 2, 512, D)
    zc = np.zeros((2, 512, D), np.float32)
    in_maps = []
    for core in range(8):
        m = dict(shared)
        if core < 4:
            m["xin"] = np.ascontiguousarray(xp[core * 4:(core + 1) * 4].reshape(NT, D))
            m["cond"] = np.ascontiguousarray(cctx.reshape(8, 128))
            m["flag"] = np.zeros((128, 1), np.float32)
            m["rpbr"] = rpbr_p; m["cmask"] = cm_p; m["pmadd"] = pm_p
            m["cmk"] = np.full((128, 1), -30000.0, np.float32)
            m["ck"] = zc; m["cv"] = zc
            m["s0f"] = zs; m["s0b"] = zs
        else:
            m["xin"] = np.ascontiguousarray(xs[core - 4])
            m["cond"] = np.ascontiguousarray(c[core - 4].reshape(8, 128))
            m["flag"] = np.ones((128, 1), np.float32)
            m["rpbr"] = rpbr_s; m["cmask"] = cm_s; m["pmadd"] = pm_s
            m["cmk"] = np.zeros((128, 1), np.float32)
            m["ck"] = np.ascontiguousarray(ck[core - 4]); m["cv"] = np.ascontiguousarray(cv[core - 4])
            m["s0f"] = np.ascontiguousarray(sgf[core - 4]); m["s0b"] = np.ascontiguousarray(sgb[core - 4])
        in_maps.append(m)
    res = run_bass_kernel_spmd(nc, in_maps, core_ids=list(range(8)))
    return res.results


def kernel(**inputs):
    cfg = {"mixers": (0, 1, 2), "nlayers": DEPTH, "astage": 3}
    r = _run(inputs, cfg)
    y_prompt = np.stack([r[cidx]["yout"] for cidx in range(4)], 0).reshape(16, 256, D)
    y_sample = np.stack([r[cidx]["yout"] for cidx in range(4, 8)], 0)
    nk = np.stack([r[cidx]["nk"] for cidx in range(4)], 0)
    nv = np.stack([r[cidx]["nv"] for cidx in range(4)], 0)
    new_k = nk.reshape(4, 2, 4, 256, 16, 64).transpose(0, 2, 1, 3, 4, 5).reshape(16, 2, 256, 16, 64)
    new_v = nv.reshape(4, 2, 4, 256, 16, 64).transpose(0, 2, 1, 3, 4, 5).reshape(16, 2, 256, 16, 64)
    new_sf = np.stack([r[cidx]["gsf"] for cidx in range(4)], 0).reshape(16, 1, 4, 128, 256)
    new_sb = np.stack([r[cidx]["gsb"] for cidx in range(4)], 0).reshape(16, 1, 4, 128, 256)
    f = lambda a: np.ascontiguousarray(a, dtype=np.float32)
    return (f(y_prompt), f(y_sample), f(new_k), f(new_v), f(new_sf), f(new_sb))
```

```python
from contextlib import ExitStack
import numpy as np
import concourse.bass as bass
import concourse.mybir as mybir
from concourse.bass_utils import run_bass_kernel_spmd

F32 = mybir.dt.float32
BF16 = mybir.dt.bfloat16
AF = mybir.ActivationFunctionType
ALU = mybir.AluOpType
AX = mybir.AxisListType

D = 1024
NT = 1024
DFF = 2816
NPAIR = DFF // 128
DEPTH = 4
EPS = 1e-6


class Buf:
    __slots__ = ("name", "w", "r", "dsem", "dcnt")

    def __init__(self, name):
        self.name = name
        self.w = None
        self.r = {}
        self.dsem = None
        self.dcnt = 0


class Eng:
    def __init__(self, name, h, sem):
        self.name = name
        self.h = h
        self.sem = sem
        self.seq = 0
        self.waited = {}


class KB:
    def __init__(self, nc, es):
        self.nc = nc
        self.es = es
        self.eng = {}
        for n, h in (("pe", nc.tensor), ("act", nc.scalar), ("dve", nc.vector), ("pool", nc.gpsimd), ("sp", nc.sync)):
            self.eng[n] = Eng(n, h, es.enter_context(nc.semaphore("sem_" + n)))
        self.nbuf = 0
        self.final = []
        self.semcache = {}

    def buf(self, name=None):
        self.nbuf += 1
        return Buf(name or ("b%d" % self.nbuf))

    def _waits(self, e, reads, writes):
        need = {}

        def add(ev):
            if ev is None:
                return
            s, v = ev
            if need.get(s, (None, 0))[1] < v:
                need[s] = (s, v)

        for b in reads:
            add(b.w)
        for b in writes:
            add(b.w)
            for s, v in b.r.items():
                add((s, v))
        for s, v in need.values():
            if e.name == "pe" and s is e.sem:
                continue
            if e.waited.get(s, 0) < v:
                e.h.wait_ge(s, v)
                e.waited[s] = v

    def op(self, en, fn, reads=(), writes=(), inc=True):
        e = self.eng[en]
        assert inc or en == "pe"
        self._waits(e, reads, writes)
        ins = fn(e.h)
        val = e.seq + 1
        if inc:
            ins.then_inc(e.sem, 1)
            e.seq += 1
        ev = (e.sem, val)
        for b in reads:
            if b.r.get(e.sem, 0) < val:
                b.r[e.sem] = val
        for b in writes:
            b.w = ev
            b.r = {}
        return ins

    def dma(self, en, out, in_, reads=(), writes=(), final=False):
        e = self.eng[en]
        self._waits(e, reads, writes)
        owner = writes[0] if writes else reads[0]
        st = self.semcache.get(owner.name)
        if st is None:
            st = [self.es.enter_context(self.nc.semaphore("ds_%s" % owner.name)), 0]
            self.semcache[owner.name] = st
        st[1] += 16
        owner.dsem, owner.dcnt = st[0], st[1]
        e.h.dma_start(out=out, in_=in_).then_inc(owner.dsem, 16)
        ev = (owner.dsem, owner.dcnt)
        for b in reads:
            if b.r.get(owner.dsem, 0) < owner.dcnt:
                b.r[owner.dsem] = owner.dcnt
        for b in writes:
            b.w = ev
            b.r = {}
        if final:
            self.final.append((en, ev))

    def finish(self):
        for en, (s, v) in self.final:
            e = self.eng[en]
            if e.waited.get(s, 0) < v:
                e.h.wait_ge(s, v)
                e.waited[s] = v


def build_program(cfg):
    nc = bass.Bass("TRN2", target_bir_lowering=False)
    mixers = cfg.get("mixers", (0, 1, 2))
    nlayers = cfg.get("nlayers", DEPTH)
    dbg = cfg.get("dbg", False)
    astage = cfg.get("astage", 3)

    def din(name, shape):
        return nc.dram_tensor(name, list(shape), F32, kind="ExternalInput").ap()

    def dout(name, shape):
        return nc.dram_tensor(name, list(shape), F32, kind="ExternalOutput").ap()

    xin = din("xin", (NT, D))
    cond = din("cond", (8, 128))
    flag = din("flag", (128, 1))
    identd = din("ident", (128, 128))
    mod_w = din("mod_w", (DEPTH, D, 6 * D))
    mod_b = din("mod_b", (DEPTH, 6 * D))
    norm1_g = din("norm1_g", (DEPTH, D))
    norm2_g = din("norm2_g", (DEPTH, D))
    ffn_w_up = din("ffn_w_up", (DEPTH, D, 2 * DFF))
    ffn_b_up = din("ffn_b_up", (DEPTH, 2 * DFF))
    ffn_w_dw = din("ffn_w_dw", (DEPTH, 3, 2 * DFF))
    ffn_b_dw = din("ffn_b_dw", (DEPTH, 2 * DFF))
    ffn_w_down = din("ffn_w_down", (DEPTH, DFF, D))
    ffn_b_down = din("ffn_b_down", (DEPTH, D))
    conv_w_pw1 = din("conv_w_pw1", (1, D, 2 * D))
    conv_b_pw1 = din("conv_b_pw1", (1, 2 * D))
    conv_w_dw = din("conv_w_dw", (1, 31, D))
    conv_b_dw = din("conv_b_dw", (1, D))
    conv_ln_g = din("conv_ln_g", (1, D))
    conv_ln_b = din("conv_ln_b", (1, D))
    conv_w_pw2 = din("conv_w_pw2", (1, D, D))
    conv_b_pw2 = din("conv_b_pw2", (1, D))
    attn_w_qkv = din("attn_w_qkv", (2, D, 3 * D))
    attn_w_o = din("attn_w_o", (2, D, D))
    attn_q_norm = din("attn_q_norm", (2, 64))
    attn_k_norm = din("attn_k_norm", (2, 64))
    rpbr_t = nc.dram_tensor("rpbr", [2, 16, 15, 127], F32, kind="ExternalInput")
    cmask = din("cmask", (128, 64))
    j2d = din("j2", (128, 128))
    blkd = din("blk", (128, 128))
    rowsel_d = nc.dram_tensor("rowsel", [16, 1024], BF16, kind="ExternalInput").ap()
    pmadd_d = nc.dram_tensor("pmadd", [16, 1024], BF16, kind="ExternalInput").ap()
    cmk = din("cmk", (128, 1))
    ck_d = din("ck", (2, 512, D))
    cv_d = din("cv", (2, 512, D))
    nk_d = dout("nk", (2, NT, D))
    nv_d = dout("nv", (2, NT, D))
    gla_w_q = din("gla_w_q", (1, D, 512))
    gla_w_k = din("gla_w_k", (1, D, 512))
    gla_w_v = din("gla_w_v", (1, D, D))
    gla_w_g = din("gla_w_g", (1, D, D))
    gla_w_gk1 = din("gla_w_gk1", (1, 2, D, 16))
    gla_w_gk2 = din("gla_w_gk2", (1, 2, 16, 512))
    gla_b_gk = din("gla_b_gk", (1, 2, 512))
    gla_o_norm = din("gla_o_norm", (1, 256))
    gla_w_o = din("gla_w_o", (1, D, D))
    s0f_d = din("s0f", (4, 128, 256))
    s0b_d = din("s0b", (4, 128, 256))
    maskf_d = din("maskf", (128, 128))
    maskb_d = din("maskb", (128, 128))
    gsf_d = dout("gsf", (4, 4, 128, 256))
    gsb_d = dout("gsb", (4, 4, 128, 256))
    yout = dout("yout", (NT, D))

    with ExitStack() as es:
        K = KB(nc, es)

        def sb(name, shape, dt):
            return es.enter_context(nc.sbuf_tensor(name, list(shape), dt))

        XT = sb("XT", (128, 8, NT), F32)
        XTb = [[K.buf("XT%d_%d" % (k, h)) for h in range(2)] for k in range(8)]
        HT = sb("HT", (128, 8, NT), BF16)
        HTb = [[K.buf("HT%d_%d" % (k, h)) for h in range(2)] for k in range(8)]
        MT = sb("MT", (128, NPAIR, NT), BF16)
        MTb = [K.buf("MT%d" % k) for k in range(NPAIR)]
        WK = [sb("WK%d" % i, (128, NT), F32) for i in range(8)]
        WKb = [K.buf("WK%d" % i) for i in range(8)]
        NWR = 3
        WR = [sb("WR%d" % i, (128, 6144), BF16) for i in range(NWR)]
        WRb = [K.buf("WR%d" % i) for i in range(NWR)]
        wr_next = [0]
        IDENT = sb("IDENT", (128, 128), F32)
        IDb = K.buf("CONSTS")
        ONESB = sb("ONESB", (128, 128), BF16)
        ONb = K.buf("ONESB")
        ONE1 = sb("ONE1", (1, 2), F32)
        O1b = K.buf("ONE1")
        FLAG = sb("FLAG", (128, 1), F32)
        FLb = IDb
        STG = [sb("STG%d" % i, (128, 128), F32) for i in range(3)]
        STGb = [K.buf("STG%d" % i) for i in range(3)]
        NVT = 8 + 8 + 48 + 44 * 5 + 8 + 8 + 8 + 8 + 44 * 2 + 8
        VT = sb("VT", (128, 1024), F32)
        VTb = K.buf("VT")
        SC = sb("SC", (128, 8), BF16)
        SCb = K.buf("SC")
        CNDT = sb("CNDT", (128, 8), F32)
        RSTD = sb("RSTD", (128, NT), F32)
        RSb = [K.buf("RSTD%d" % h) for h in range(2)]
        SQ = [sb("SQ%d" % i, (128, 512), BF16) for i in range(4)]
        SQb = [K.buf("SQ%d" % i) for i in range(4)]
        PAIR2 = sb("PAIR2", (128, 5632 + 960), BF16)
        PAIR2_bufs = []
        EBT2 = sb("EBT2", (128, 2, 1984), BF16)
        sq_next = [0]
        PS = [es.enter_context(nc.psum_tensor("PS%d" % i, [128, 512], F32)) for i in range(8)]
        PSb = [K.buf("PS%d" % i) for i in range(8)]
        ps_next = [0]
        ps_range = [0, 7]

        ps_banks = [list(range(7))]

        def ps():
            banks = ps_banks[0]
            i = ps_next[0] % len(banks)
            ps_next[0] = i + 1
            b = banks[i]
            return PS[b], PSb[b]

        def wslot():
            i = wr_next[0]
            wr_next[0] = (i + 1) % NWR
            return WR[i], WRb[i]

        col = {}
        c0 = 0
        for nm, n in (("g1", 8), ("g2", 8), ("bdown", 8), ("bup", 44), ("bdw", 44), ("w0", 44), ("w1", 44),
                      ("w2", 44), ("mod", 192), ("modb", 48), ("A1", 8), ("A2", 8), ("g2bd", 8), ("fw0", 44), ("fw2", 44), ("g1b", 8), ("cv", 48), ("cw", 248), ("gv", 16)):
            col[nm] = c0
            c0 += n
        assert c0 <= 1024

        cur_mod = [0]

        def vt(nm, j=0, n=1):
            if nm == "mod":
                j = j + cur_mod[0]
            return VT[:, col[nm] + j: col[nm] + j + n]

        K.dma("sp", IDENT[:], identd[:, :], writes=[IDb])
        K.dma("sp", FLAG[:], flag[:, :], writes=[FLb])
        K.op("dve", lambda h: h.memset(ONESB[:], 1.0 / D), writes=[ONb])
        K.op("dve", lambda h: h.memset(ONE1[:], 1.0), writes=[O1b])

        for tt in range(8):
            st, stb = WK[tt % 4], WKb[tt % 4]
            K.dma("sp", st[:], xin[tt * 128:(tt + 1) * 128, :], writes=[stb])
            for g in range(2):
                p, pb = ps()
                for kk in range(4):
                    k = g * 4 + kk
                    K.op("pe", lambda h, k=k, kk=kk, p=p, st=st: h.transpose(p[:, kk * 128:(kk + 1) * 128], st[:, k * 128:(k + 1) * 128], IDENT[:]),
                         reads=[stb, IDb], writes=[pb], inc=(kk == 3))
                hf = tt // 4
                K.op("dve" if g == 0 else "act",
                     (lambda h, p=p, g=g, tt=tt: h.tensor_copy(XT[:, g * 4:(g + 1) * 4, tt * 128:(tt + 1) * 128], p[:].rearrange("p (k n) -> p k n", k=4))) if g == 0 else
                     (lambda h, p=p, g=g, tt=tt: h.copy(XT[:, g * 4:(g + 1) * 4, tt * 128:(tt + 1) * 128], p[:].rearrange("p (k n) -> p k n", k=4))),
                     reads=[pb], writes=[XTb[k][hf] for k in range(g * 4, g * 4 + 4)])

        K.dma("sp", STG[0][0:8, :], cond[:, :], writes=[STGb[0]])
        p, pb = ps()
        K.op("pe", lambda h: h.transpose(p[:, 0:8], STG[0][0:8, :], IDENT[0:8, 0:8]), reads=[STGb[0], IDb], writes=[pb])
        K.op("act", lambda h: h.activation(out=SC[:], in_=p[:, 0:8], func=AF.Silu), reads=[pb], writes=[SCb])

        def vec_rows(stg_i, row0, src_ap, n):
            K.dma("sp", STG[stg_i][row0:row0 + n, :], src_ap, writes=[STGb[stg_i]])

        def stg_to_vt(stg_i, nrows, names):
            p, pb = ps()
            K.op("pe", lambda h: h.transpose(p[:, 0:nrows], STG[stg_i][0:nrows, :], IDENT[0:nrows, 0:nrows]),
                 reads=[STGb[stg_i], IDb], writes=[pb])
            c = col[names[0]]
            K.op("dve", lambda h: h.tensor_copy(VT[:, c:c + nrows], p[:, 0:nrows]), reads=[pb], writes=[VTb])


        def mod_steps(i):
            mw = mod_w[i].rearrange("(k p) n -> p k n", p=128)
            pm, pmb = PS[7], PSb[7]

            def mk(cg):
                def f():
                    w, wb = wslot()
                    wv = w[:, 0:4096].rearrange("p (k n) -> p k n", k=8)
                    K.dma("pool", wv, mw[:, :, cg * 512:(cg + 1) * 512], writes=[wb])
                    for jj in range(4):
                        jcol = cg * 4 + jj
                        for k in range(8):
                            K.op("pe", lambda h, k=k, jj=jj, jcol=jcol: h.matmul(pm[:, jcol:jcol + 1], wv[:, k, jj * 128:(jj + 1) * 128], SC[:, k:k + 1], start=(k == 0), stop=(k == 7)),
                                 reads=[SCb, wb], writes=[pmb], inc=(k == 7))
                return f

            def fin():
                vec_rows(2, 64, mod_b[i].rearrange("(c p) -> c p", p=128), 48)
                p, pb = ps()
                K.op("pe", lambda h, p=p: h.transpose(p[:, 0:48], STG[2][64:112, :], IDENT[64:112, 64:112]), reads=[STGb[2], IDb], writes=[pb])
                cmb = col["modb"]
                K.op("act", lambda h, p=p: h.copy(VT[:, cmb:cmb + 48], p[:, 0:48]), reads=[pb], writes=[VTb])
                cm = col["mod"] + 48 * i
                K.op("dve", lambda h: h.tensor_tensor(out=VT[:, cm:cm + 48], in0=pm[:, 0:48], in1=VT[:, cmb:cmb + 48], op=ALU.add), reads=[pmb, VTb], writes=[VTb])
            return [mk(cg) for cg in range(12)], fin

        def mod_compute(i):
            steps, fin = mod_steps(i)
            for st in steps:
                st()
            fin()

        def layer_vectors(i):
            r = lambda ap, n: ap.rearrange("(c p) -> c p", p=128)
            vec_rows(0, 0, r(norm1_g[i], 8), 8)
            vec_rows(0, 8, r(norm2_g[i], 8), 8)
            vec_rows(0, 16, r(ffn_b_down[i], 8), 8)
            vec_rows(0, 24, r(ffn_b_up[i], 44), 44)
            vec_rows(0, 68, r(ffn_b_dw[i], 44), 44)
            stg_to_vt(0, 112, ["g1"])
            vec_rows(1, 0, r(ffn_w_dw[i, 0], 44), 44)
            vec_rows(1, 44, r(ffn_w_dw[i, 1], 44), 44)
            stg_to_vt(1, 88, ["w0"])
            vec_rows(2, 0, r(ffn_w_dw[i, 2], 44), 44)
            stg_to_vt(2, 44, ["w2"])
            cur_mod[0] = 48 * i
            K.op("dve", lambda h: h.scalar_tensor_tensor(out=vt("A1", 0, 8), in0=vt("mod", 8, 8), scalar=1.0, in1=vt("g1", 0, 8), op0=ALU.add, op1=ALU.mult), reads=[VTb], writes=[VTb])
            K.op("dve", lambda h: h.scalar_tensor_tensor(out=vt("A2", 0, 8), in0=vt("mod", 32, 8), scalar=1.0, in1=vt("g2", 0, 8), op0=ALU.add, op1=ALU.mult), reads=[VTb], writes=[VTb])
            K.op("dve", lambda h: h.tensor_tensor(out=vt("g2bd", 0, 8), in0=vt("mod", 40, 8), in1=vt("bdown", 0, 8), op=ALU.mult), reads=[VTb], writes=[VTb])
            K.op("dve", lambda h: h.tensor_scalar(out=vt("fw0", 0, 44), in0=vt("w0", 0, 44), scalar1=FLAG[:, 0:1], scalar2=None, op0=ALU.mult), reads=[VTb, FLb], writes=[VTb])
            K.op("dve", lambda h: h.tensor_scalar(out=vt("fw2", 0, 44), in0=vt("w2", 0, 44), scalar1=FLAG[:, 0:1], scalar2=None, op0=ALU.mult), reads=[VTb, FLb], writes=[VTb])

        def norm_mod(Aname, shift_off):
            for hf in range(2):
                sl = slice(hf * 512, (hf + 1) * 512)
                p, pb = ps()
                for k in range(8):
                    qi = sq_next[0]
                    sq_next[0] = (qi + 1) % 4
                    K.op("act", lambda h, k=k, qi=qi: h.activation(out=SQ[qi][:], in_=XT[:, k, sl], func=AF.Square), reads=[XTb[k][hf]], writes=[SQb[qi]])
                    K.op("pe", lambda h, k=k, qi=qi, p=p: h.matmul(p[:], ONESB[:], SQ[qi][:], start=(k == 0), stop=(k == 7)),
                         reads=[ONb, SQb[qi]], writes=[pb], inc=True)
                K.op("act", lambda h, p=p: h.activation(out=RSTD[:, sl], in_=p[:], func=AF.Ln, bias=EPSV[:, 0:1]), reads=[pb, EPb], writes=[RSb[hf]])
                K.op("act", lambda h: h.activation(out=RSTD[:, sl], in_=RSTD[:, sl], func=AF.Exp, scale=-0.5), reads=[RSb[hf]], writes=[RSb[hf]])
                for k in range(8):
                    wi = k % 2
                    K.op("dve", lambda h, k=k, wi=wi: h.tensor_tensor(out=WK[wi][:, 0:512], in0=XT[:, k, sl], in1=RSTD[:, sl], op=ALU.mult),
                         reads=[XTb[k][hf], RSb[hf]], writes=[WKb[wi]])
                    K.op("act", lambda h, k=k, wi=wi: h.activation(out=HT[:, k, sl], in_=WK[wi][:, 0:512], func=AF.Identity, scale=vt(Aname, k), bias=vt("mod", shift_off + k)),
                         reads=[WKb[wi], VTb], writes=[HTb[k][hf]])

        EPSV = sb("EPSV", (128, 1), F32)
        EPb = K.buf("EPSV")
        K.op("dve", lambda h: h.memset(EPSV[:], EPS), writes=[EPb])

        def epilogue_tiles(tts):
            for tt in tts:
                hf = tt // 4
                st, stb = WK[4 + tt % 4], WKb[4 + tt % 4]
                for g in range(2):
                    p, pb = ps()
                    for kk in range(4):
                        k = g * 4 + kk
                        K.op("pe", lambda h, k=k, kk=kk, p=p, tt=tt: h.transpose(p[:, kk * 128:(kk + 1) * 128], XT[:, k, tt * 128:(tt + 1) * 128], IDENT[:]),
                             reads=[XTb[k][hf], IDb], writes=[pb], inc=(kk == 3))
                    if g == 0:
                        K.op("dve", lambda h, p=p, st=st: h.tensor_copy(st[:, 0:512], p[:]), reads=[pb], writes=[stb])
                    else:
                        K.op("act", lambda h, p=p, st=st: h.copy(st[:, 512:1024], p[:]), reads=[pb], writes=[stb])
                K.dma("sp", yout[tt * 128:(tt + 1) * 128, :], st[:], reads=[stb], final=True)

        def ffn(i):
            norm_mod("A2", 24)
            wup = ffn_w_up[i].rearrange("(k p) n -> p k n", p=128)
            wv = None
            msteps, mfin = mod_steps(i + 1) if i + 1 < nlayers else ([], None)
            for pr in range(NPAIR):
                if msteps and pr >= 2 and pr % 2 == 0 and (pr - 2) // 2 < len(msteps) - 2:
                    msteps[(pr - 2) // 2]()
                if pr % 3 == 0:
                    w, wb = wslot()
                    ng = min(3, NPAIR - pr)
                    wv = w[:, :].rearrange("p (k a n) -> p k a n", k=8, a=2)
                    K.dma("pool", wv[:, :, 0, 0:ng * 128], wup[:, :, pr * 128:(pr + ng) * 128], writes=[wb])
                    K.dma("pool", wv[:, :, 1, 0:ng * 128], wup[:, :, DFF + pr * 128:DFF + (pr + ng) * 128], writes=[wb])
                set_i = pr % 2
                YA, YAb = WK[set_i * 4], WKb[set_i * 4]
                YG, YGb = WK[set_i * 4 + 1], WKb[set_i * 4 + 1]
                UA, UAb = WK[set_i * 4 + 2], WKb[set_i * 4 + 2]
                UG, UGb = WK[set_i * 4 + 3], WKb[set_i * 4 + 3]
                o = (pr % 3) * 128
                for a, (Y, Yb, ch) in enumerate(((YA, YAb, pr), (YG, YGb, NPAIR + pr))):
                    for hf in range(2):
                        sl = slice(hf * 512, (hf + 1) * 512)
                        p, pb = ps()
                        for k in range(8):
                            K.op("pe", lambda h, k=k, p=p, a=a, o=o, wv=wv, sl=sl: h.matmul(p[:], wv[:, k, a, o:o + 128], HT[:, k, sl], start=(k == 0), stop=(k == 7)),
                                 reads=[wb, HTb[k][hf]], writes=[pb], inc=(k == 7))
                        K.op("act", lambda h, p=p, Y=Y, sl=sl, ch=ch: h.activation(out=Y[:, sl], in_=p[:], func=AF.Identity, bias=vt("bup", ch)),
                             reads=[pb, VTb], writes=[Yb])
                for (Y, Yb, U, Ub, ch) in ((YA, YAb, UA, UAb, pr), (YG, YGb, UG, UGb, NPAIR + pr)):
                    K.op("act", lambda h, Y=Y, U=U, ch=ch: h.activation(out=U[:], in_=Y[:], func=AF.Identity, scale=vt("w1", ch), bias=vt("bdw", ch)),
                         reads=[Yb, VTb], writes=[Ub])
                    Y3 = Y[:, :].rearrange("p (s t) -> p s t", t=256)
                    U3 = U[:, :].rearrange("p (s t) -> p s t", t=256)
                    K.op("dve", lambda h, Y3=Y3, U3=U3, ch=ch: h.scalar_tensor_tensor(out=U3[:, :, 1:256], in0=Y3[:, :, 0:255], scalar=vt("w0", ch), in1=U3[:, :, 1:256], op0=ALU.mult, op1=ALU.add),
                         reads=[Yb, VTb, Ub], writes=[Ub])
                    K.op("dve", lambda h, Y3=Y3, U3=U3, ch=ch: h.scalar_tensor_tensor(out=U3[:, :, 0:255], in0=Y3[:, :, 1:256], scalar=vt("w2", ch), in1=U3[:, :, 0:255], op0=ALU.mult, op1=ALU.add),
                         reads=[Yb, VTb, Ub], writes=[Ub])
                    K.op("dve", lambda h, Y3=Y3, U3=U3, ch=ch: h.scalar_tensor_tensor(out=U3[:, 1:4, 0], in0=Y3[:, 0:3, 255], scalar=vt("fw0", ch), in1=U3[:, 1:4, 0], op0=ALU.mult, op1=ALU.add),
                         reads=[Yb, VTb, Ub], writes=[Ub])
                    K.op("dve", lambda h, Y3=Y3, U3=U3, ch=ch: h.scalar_tensor_tensor(out=U3[:, 0:3, 255], in0=Y3[:, 1:4, 0], scalar=vt("fw2", ch), in1=U3[:, 0:3, 255], op0=ALU.mult, op1=ALU.add),
                         reads=[Yb, VTb, Ub], writes=[Ub])
                K.op("act", lambda h, UG=UG: h.activation(out=UG[:], in_=UG[:], func=AF.Silu), reads=[UGb], writes=[UGb])
                K.op("dve", lambda h, UA=UA, UG=UG, pr=pr: h.tensor_tensor(out=MT[:, pr, :], in0=UA[:], in1=UG[:], op=ALU.mult), reads=[UAb, UGb], writes=[MTb[pr]])
            if mfin is not None:
                for st in msteps[10:]:
                    st()
                mfin()
            wdn = ffn_w_down[i].rearrange("(k p) n -> p k n", p=128)
            for hf in range(2):
                sl = slice(hf * 512, (hf + 1) * 512)
                for n2 in range(4):
                    w, wb = wslot()
                    wv = w[:, 0:NPAIR * 256].rearrange("p (k n) -> p k n", k=NPAIR)
                    K.dma("pool", wv, wdn[:, :, n2 * 256:(n2 + 1) * 256], writes=[wb])
                    for nn in range(2):
                        c = n2 * 2 + nn
                        p, pb = ps()
                        for k in range(NPAIR):
                            K.op("pe", lambda h, k=k, p=p, nn=nn, wv=wv, sl=sl: h.matmul(p[:], wv[:, k, nn * 128:(nn + 1) * 128], MT[:, k, sl], start=(k == 0), stop=(k == NPAIR - 1)),
                                 reads=[wb, MTb[k]], writes=[pb], inc=(k == NPAIR - 1))
                        wi = 2 + (c % 2)
                        K.op("act", lambda h, p=p, wi=wi, c=c: h.activation(out=WK[wi][:, 0:512], in_=p[:], func=AF.Identity, scale=vt("mod", 40 + c), bias=vt("g2bd", c)),
                             reads=[pb, VTb], writes=[WKb[wi]])
                        K.op("dve", lambda h, wi=wi, c=c, sl=sl: h.tensor_tensor(out=XT[:, c, sl], in0=XT[:, c, sl], in1=WK[wi][:, 0:512], op=ALU.add),
                             reads=[WKb[wi], XTb[c][hf]], writes=[XTb[c][hf]])
                if i == nlayers - 1:
                    epilogue_tiles(range(hf * 4, hf * 4 + 4))

        IDENTB = sb("IDENTB", (128, 128), BF16)
        IDBb = K.buf("IDENTB")
        K.op("dve", lambda h: h.tensor_copy(IDENTB[:], IDENT[:]), reads=[IDb], writes=[IDBb])
        MTflat = MT[:, :, :].rearrange("p k n -> p (k n)")
        MTf32 = MTflat.bitcast(F32)

        def alias(dst, src):
            ev = {}
            for b in src:
                if b.w is not None:
                    s_, v_ = b.w
                    ev[s_] = max(ev.get(s_, 0), v_)
                for s_, v_ in b.r.items():
                    ev[s_] = max(ev.get(s_, 0), v_)
            for b in dst:
                b.w = None
                b.r = dict(ev)

        def out_proj(wdram, kchunks, rhs_fn, rhs_bufs_fn, bias_name):
            wv_d = wdram.rearrange("(k p) n -> p k n", p=128)
            per = 8192 // (kchunks * 128) * 128
            per = min(per, 512)
            nslots = D // per
            for hf in range(2):
                sl = slice(hf * 512, (hf + 1) * 512)
                for sI in range(nslots):
                    w, wb = wslot()
                    wv = w[:, 0:kchunks * per].rearrange("p (k n) -> p k n", k=kchunks)
                    K.dma("pool", wv, wv_d[:, :, sI * per:(sI + 1) * per], writes=[wb])
                    for nn in range(per // 128):
                        c = sI * (per // 128) + nn
                        p, pb = ps()
                        for k in range(kchunks):
                            K.op("pe", lambda h, k=k, p=p, nn=nn, wv=wv, sl=sl: h.matmul(p[:], wv[:, k, nn * 128:(nn + 1) * 128], rhs_fn(k, sl), start=(k == 0), stop=(k == kchunks - 1)),
                                 reads=[wb] + rhs_bufs_fn(k, hf), writes=[pb], inc=(k == kchunks - 1))
                        wi = 2 + (c % 2)
                        K.op("act", lambda h, p=p, wi=wi, c=c: h.activation(out=WK[wi][:, 0:512], in_=p[:], func=AF.Identity, scale=vt("mod", 16 + c), bias=vt("g1b", c)),
                             reads=[pb, VTb], writes=[WKb[wi]])
                        K.op("dve", lambda h, wi=wi, c=c, sl=sl: h.tensor_tensor(out=XT[:, c, sl], in0=XT[:, c, sl], in1=WK[wi][:, 0:512], op=ALU.add),
                             reads=[WKb[wi], XTb[c][hf]], writes=[XTb[c][hf]])

        def conv_mixer(i, j):
            norm_mod("A1", 0)
            r = lambda ap: ap.rearrange("(c p) -> c p", p=128)
            vec_rows(0, 0, r(conv_b_pw1[j]), 16)
            vec_rows(0, 16, r(conv_b_dw[j]), 8)
            vec_rows(0, 24, r(conv_ln_g[j]), 8)
            vec_rows(0, 32, r(conv_ln_b[j]), 8)
            vec_rows(0, 40, r(conv_b_pw2[j]), 8)
            stg_to_vt(0, 48, ["cv"])
            K.op("dve", lambda h: h.tensor_tensor(out=vt("g1b", 0, 8), in0=vt("mod", 16, 8), in1=vt("cv", 40, 8), op=ALU.mult), reads=[VTb], writes=[VTb])
            UPW = 286
            UPb = [K.buf("UP%d" % c) for c in range(8)]
            TMb = [K.buf("TM%d" % t) for t in range(4)]
            alias(UPb + TMb, MTb)
            UPall = MTflat[:, 0:8 * 4 * UPW]
            UP = [MTflat[:, c * 4 * UPW:(c + 1) * 4 * UPW].rearrange("p (s t) -> p s t", s=4) for c in range(8)]
            tm0 = (8 * 4 * UPW * 2 + 3) // 4
            TM = [MTf32[:, tm0 + t * 1024: tm0 + (t + 1) * 1024] for t in range(4)]
            K.op("pool", lambda h: h.memset(UPall, 0.0), writes=UPb)
            K.dma("sp", TM[3][0:31, :], conv_w_dw[j], writes=[TMb[3]])
            p, pb = ps()
            for c in range(8):
                K.op("pe", lambda h, c=c, p=p: h.transpose(p[:, c * 31:(c + 1) * 31], TM[3][0:31, c * 128:(c + 1) * 128], IDENT[0:31, 0:31]),
                     reads=[TMb[3], IDb], writes=[pb], inc=(c == 7))
            ccw = col["cw"]
            K.op("dve", lambda h, p=p: h.tensor_copy(VT[:, ccw:ccw + 248], p[:, 0:248]), reads=[pb], writes=[VTb])
            w1d = conv_w_pw1[j].rearrange("(k p) n -> p k n", p=128)
            wv = None
            for c in range(8):
                if c % 2 == 0:
                    w, wb = wslot()
                    wv = w[:, 0:4096].rearrange("p (k a n) -> p k a n", k=8, a=2)
                    K.dma("pool", wv[:, :, 0, :], w1d[:, :, c * 128:(c + 2) * 128], writes=[wb])
                    K.dma("pool", wv[:, :, 1, :], w1d[:, :, D + c * 128:D + (c + 2) * 128], writes=[wb])
                o = (c % 2) * 128
                for hf in range(2):
                    sl = slice(hf * 512, (hf + 1) * 512)
                    pa, pab = ps()
                    pg, pgb = ps()
                    for a, (p, pb) in enumerate(((pa, pab), (pg, pgb))):
                        for k in range(8):
                            K.op("pe", lambda h, k=k, p=p, a=a, o=o, wv=wv, sl=sl: h.matmul(p[:], wv[:, k, a, o:o + 128], HT[:, k, sl], start=(k == 0), stop=(k == 7)),
                                 reads=[wb, HTb[k][hf]], writes=[pb], inc=(k == 7))
                    ta, tab = TM[hf * 2], TMb[hf * 2]
                    tg, tgb = TM[hf * 2 + 1], TMb[hf * 2 + 1]
                    K.op("act", lambda h, pa=pa, ta=ta, c=c: h.activation(out=ta[:, 0:512], in_=pa[:], func=AF.Identity, bias=vt("cv", c)), reads=[pab, VTb], writes=[tab])
                    K.op("act", lambda h, pg=pg, tg=tg, c=c: h.activation(out=tg[:, 0:512], in_=pg[:], func=AF.Sigmoid, bias=vt("cv", 8 + c)), reads=[pgb, VTb], writes=[tgb])
                    K.op("dve", lambda h, ta=ta, tg=tg, c=c, hf=hf: h.tensor_tensor(out=UP[c][:, 2 * hf:2 * hf + 2, 15:271], in0=ta[:, 0:512].rearrange("p (s t) -> p s t", s=2),
                                                                                 in1=tg[:, 0:512].rearrange("p (s t) -> p s t", s=2), op=ALU.mult),
                         reads=[tab, tgb], writes=[UPb[c]])
                K.op("dve", lambda h, c=c: h.tensor_scalar(out=UP[c][:, 1:4, 0:15], in0=UP[c][:, 0:3, 256:271], scalar1=FLAG[:, 0:1], scalar2=None, op0=ALU.mult), reads=[UPb[c], FLb], writes=[UPb[c]])
                K.op("dve", lambda h, c=c: h.tensor_scalar(out=UP[c][:, 0:3, 271:286], in0=UP[c][:, 1:4, 15:30], scalar1=FLAG[:, 0:1], scalar2=None, op0=ALU.mult), reads=[UPb[c], FLb], writes=[UPb[c]])
            alias(WKb, WKb)
            for c in range(8):
                dg, dgb = wslot()
                K.op("dve", lambda h, c=c, dg=dg: h.tensor_tensor(out=dg[:, 0:31 * 128].rearrange("p (k n) -> p k n", k=31), in0=IDENTB[:, :].unsqueeze(1).broadcast_to([128, 31, 128]),
                                                              in1=vt("cw", c * 31, 31).unsqueeze(2).broadcast_to([128, 31, 128]), op=ALU.mult), reads=[IDBb, VTb], writes=[dgb])
                for hf in range(2):
                    p, pb = ps()
                    for k in range(31):
                        K.op("pe", lambda h, k=k, c=c, p=p, dg=dg, hf=hf: h.matmul(p[:].rearrange("p (s t) -> p s t", s=2), dg[:, k * 128:(k + 1) * 128], UP[c][:, 2 * hf:2 * hf + 2, k:k + 256], start=(k == 0), stop=(k == 30)),
                             reads=[dgb, UPb[c]], writes=[pb], inc=(k == 30))
                    K.op("act", lambda h, p=p, c=c, hf=hf: h.activation(out=WK[c][:, hf * 512:(hf + 1) * 512], in_=p[:], func=AF.Identity, bias=vt("cv", 16 + c)), reads=[pb, VTb], writes=[WKb[c]])
            for hf in range(2):
                sl = slice(hf * 512, (hf + 1) * 512)
                pm_, pmb_ = ps()
                pq, pqb = ps()
                for c in range(8):
                    qi = sq_next[0]; sq_next[0] = (qi + 1) % 4
                    K.op("act", lambda h, c=c, qi=qi: h.copy(SQ[qi][:], WK[c][:, sl]), reads=[WKb[c]], writes=[SQb[qi]])
                    K.op("pe", lambda h, c=c, qi=qi, p=pm_: h.matmul(p[:], ONESB[:], SQ[qi][:], start=(c == 0), stop=(c == 7)), reads=[ONb, SQb[qi]], writes=[pmb_])
                    qi = sq_next[0]; sq_next[0] = (qi + 1) % 4
                    K.op("act", lambda h, c=c, qi=qi: h.activation(out=SQ[qi][:], in_=WK[c][:, sl], func=AF.Square), reads=[WKb[c]], writes=[SQb[qi]])
                    K.op("pe", lambda h, c=c, qi=qi, p=pq: h.matmul(p[:], ONESB[:], SQ[qi][:], start=(c == 0), stop=(c == 7)), reads=[ONb, SQb[qi]], writes=[pqb])
                mu, mub = TM[0], TMb[0]
                var, varb = TM[1], TMb[1]
                K.op("act", lambda h: h.copy(mu[:, 0:512], pm_[:]), reads=[pmb_], writes=[mub])
                K.op("dve", lambda h: h.tensor_tensor(out=var[:, 0:512], in0=mu[:, 0:512], in1=mu[:, 0:512], op=ALU.mult), reads=[mub], writes=[varb])
                K.op("dve", lambda h: h.tensor_tensor(out=var[:, 0:512], in0=pq[:], in1=var[:, 0:512], op=ALU.subtract), reads=[pqb, varb], writes=[varb])
                K.op("act", lambda h: h.activation(out=RSTD[:, sl], in_=var[:, 0:512], func=AF.Ln, bias=EPSV[:, 0:1]), reads=[varb, EPb], writes=[RSb[hf]])
                K.op("act", lambda h: h.activation(out=RSTD[:, sl], in_=RSTD[:, sl], func=AF.Exp, scale=-0.5), reads=[RSb[hf]], writes=[RSb[hf]])
                for c in range(8):
                    t2, t2b = TM[2 + (c % 2)], TMb[2 + (c % 2)]
                    K.op("dve", lambda h, c=c, t2=t2: h.tensor_tensor(out=t2[:, 0:512], in0=WK[c][:, sl], in1=mu[:, 0:512], op=ALU.subtract), reads=[WKb[c], mub], writes=[t2b])
                    K.op("dve", lambda h, c=c, t2=t2: h.tensor_tensor(out=t2[:, 0:512], in0=t2[:, 0:512], in1=RSTD[:, sl], op=ALU.mult), reads=[t2b, RSb[hf]], writes=[t2b])
                    K.op("act", lambda h, c=c, t2=t2: h.activation(out=HT[:, c, sl], in_=t2[:, 0:512], func=AF.Silu, scale=vt("cv", 24 + c), bias=vt("cv", 32 + c)),
                         reads=[t2b, VTb], writes=[HTb[c][hf]])
            out_proj(conv_w_pw2[j], 8, lambda k, sl: HT[:, k, sl], lambda k, hf: [HTb[k][hf]], "cv")
            alias(MTb, UPb + TMb)


        ACON = sb("ACON", (128, 64 + 128 + 1 + 2), F32)
        ACb = IDb
        K.dma("sp", ACON[:, 0:64], cmask[:, :], writes=[ACb])
        K.dma("sp", ACON[:, 192:193], cmk[:, :], writes=[ACb])
        J2B = sb("J2B", (128, 128), BF16)
        BLKB = sb("BLKB", (128, 128), BF16)
        ONE1B = sb("ONE1B", (128, 128), BF16)
        ABb = K.buf("ACONB")
        K.dma("pool", J2B[:], j2d[:, :], writes=[ABb])
        K.dma("pool", BLKB[:], blkd[:, :], writes=[ABb])
        K.op("dve", lambda h: h.memset(ONE1B[:], 1.0), writes=[ABb])

        def attn_mixer(i, j):
            norm_mod("A1", 0)
            r = lambda ap: ap.rearrange("(c p) -> c p", p=128)
            K.op("dve", lambda h: h.memset(vt("g1b", 0, 8), 0.0), writes=[VTb])
            for half in range(2):
                K.dma("sp", ACON[half * 64:(half + 1) * 64, 193:194], attn_q_norm[j].rearrange("(p o) -> p o", o=1), writes=[ACb])
                K.dma("sp", ACON[half * 64:(half + 1) * 64, 194:195], attn_k_norm[j].rearrange("(p o) -> p o", o=1), writes=[ACb])
            K.op("dve", lambda h: h.tensor_scalar(out=ACON[:, 193:194], in0=ACON[:, 193:194], scalar1=0.125, scalar2=None, op0=ALU.mult), reads=[ACb], writes=[ACb])
            names = ["VTOK", "OT", "QTE", "QTO", "KE", "KO", "KCE", "KCO", "CVP"]
            sizes = [8192, 8192, 1024, 1024, 1024, 1024, 512, 512, 512]
            reg_E = PAIR2[:, 5632:5632 + 960]
            reg = {}
            o = 0
            for nm, sz in zip(names, sizes):
                reg[nm] = MTflat[:, o:o + sz]
                o += sz
            assert o <= NPAIR * NT
            VTOK = reg["VTOK"].rearrange("p (t n) -> p t n", t=8)
            OT = reg["OT"].rearrange("p (c n) -> p c n", c=8)
            QTE, QTO, KE, KO, KCE, KCO = reg["QTE"], reg["QTO"], reg["KE"], reg["KO"], reg["KCE"], reg["KCO"]
            CVP = reg["CVP"].rearrange("p (t n) -> p t n", t=4)
            E = reg_E.rearrange("p (m q) -> p m q", m=15)
            VTOKb = [K.buf("VTOK%d" % t) for t in range(8)]
            OTb = [[K.buf("OT%d_%d" % (c, h)) for h in range(2)] for c in range(8)]
            QTEb, QTOb, KEb, KOb, KCEb, KCOb, CVPb, Eb = [K.buf(n) for n in ("QTE", "QTO", "KE", "KO", "KCE", "KCO", "CVP", "E")]
            allm = VTOKb + [b for l in OTb for b in l] + [QTEb, QTOb, KEb, KOb, KCEb, KCOb, CVPb]
            alias(allm, MTb)
            RP = WK[0][:, 0:960].rearrange("p (m q) -> p m q", m=15)
            EBT = [WK[1 + par][:, :].bitcast(BF16)[:, 0:1984].rearrange("p (m q) -> p m q", m=31) for par in range(2)]
            PTt = [WK[3][:, :].bitcast(BF16)[:, t * 512:(t + 1) * 512] for t in range(4)]
            Tt, Rr = WK[4][:, 0:512], WK[4][:, 512:1024]
            TN, KST = WK[5][:, 0:512], WK[5][:, 512:1024]
            CKP, RD = WK[6][:, 0:512].rearrange("p (t n) -> p t n", t=4), WK[6][:, 512:1024]
            VST = [WK[7][:, 0:512], WK[7][:, 512:1024]]
            RPb, PTb, Tb, Rb, TNb, KSTb, CKPb, RDb = K.buf("RP"), [K.buf("PT%d" % t) for t in range(4)], K.buf("T"), K.buf("R"), K.buf("TN"), K.buf("KST"), K.buf("CKP"), K.buf("RD")
            EBTb = [K.buf("EBT0"), K.buf("EBT1")]
            VSTb = [K.buf("VST0"), K.buf("VST1")]
            allw = [RPb, Tb, Rb, TNb, KSTb, CKPb, RDb] + PTb + EBTb + VSTb
            alias(allw, WKb)
            wq = attn_w_qkv[j].rearrange("(k p) n -> p k n", p=128)
            for ch in range(2):
                w, wb = wslot()
                wv = w[:, 0:4096].rearrange("p (k n) -> p k n", k=8)
                K.dma("pool", wv, wq[:, :, 2 * D + ch * 512:2 * D + (ch + 1) * 512], writes=[wb])
                for tt in range(8):
                    p, pb = ps()
                    for k in range(8):
                        K.op("pe", lambda h, k=k, p=p, wv=wv, tt=tt: h.matmul(p[:], HT[:, k, tt * 128:(tt + 1) * 128], wv[:, k, :], start=(k == 0), stop=(k == 7)),
                             reads=[wb, HTb[k][tt // 4]], writes=[pb], inc=(k == 7))
                    vs, vsb = VST[tt % 2], VSTb[tt % 2]
                    K.op("act", lambda h, p=p, vs=vs: h.copy(vs, p[:]), reads=[pb], writes=[vsb])
                    K.op("dve", lambda h, vs=vs, tt=tt, ch=ch: h.tensor_copy(VTOK[:, tt, ch * 512:(ch + 1) * 512], vs), reads=[vsb], writes=[VTOKb[tt]])
                    K.dma("sp", nv_d[j, tt * 128:(tt + 1) * 128, ch * 512:(ch + 1) * 512], vs, reads=[vsb], final=True)
            BS = []
            o2 = 0
            for si in range(2):
                d = {}
                if si == 0:
                    d.update(QTE=QTE, QTO=QTO, KE=KE, KO=KO, KCE=KCE, KCO=KCO, CVP=CVP, EBT=EBT,
                             QTEb=QTEb, QTOb=QTOb, KEb=KEb, KOb=KOb, KCEb=KCEb, KCOb=KCOb, CVPb=CVPb, EBTb=EBTb)
                else:
                    def take(n):
                        nonlocal_o = take.o
                        take.o += n
                        return PAIR2[:, nonlocal_o:nonlocal_o + n]
                    take.o = 0
                    d.update(QTE=take(1024), QTO=take(1024), KE=take(1024), KO=take(1024), KCE=take(512), KCO=take(512))
                    d["CVP"] = take(512).rearrange("p (t n) -> p t n", t=4)
                    d["EBT"] = [EBT2[:, par, :].rearrange("p (m q) -> p m q", m=31) for par in range(2)]
                    for nm in ("QTE", "QTO", "KE", "KO", "KCE", "KCO", "CVP"):
                        d[nm + "b"] = K.buf(nm + "_2")
                    d["EBTb"] = [K.buf("EBT0_2"), K.buf("EBT1_2")]
                BS.append(d)
            del PAIR2_bufs[:]
            PAIR2_bufs.extend([BS[1][n_ + "b"] for n_ in ("QTE", "QTO", "KE", "KO", "KCE", "KCO", "CVP")])
            for d in BS:
                K.op("pool", lambda h, d=d: h.memset(d["KE"], 0.0), writes=[d["KEb"]])
                K.op("pool", lambda h, d=d: h.memset(d["KO"], 0.0), writes=[d["KOb"]])
                K.dma("sp", d["KE"][64:80, :], rowsel_d[:, :], writes=[d["KEb"]])
                K.dma("sp", d["KO"][0:16, :], rowsel_d[:, :], writes=[d["KOb"]])
                K.op("pool", lambda h, d=d: h.memset(d["KCE"], 0.0), writes=[d["KCEb"]])
                K.op("pool", lambda h, d=d: h.memset(d["KCO"], 0.0), writes=[d["KCOb"]])
                for par in range(2):
                    K.op("pool", lambda h, d=d, par=par: h.memset(d["EBT"][par], 0.0), writes=[d["EBTb"][par]])
            Tq, Tk = WK[4], WK[5]
            Rq, Rk = WK[7], RSTD
            KSTs = [WK[4][:, 0:512], WK[4][:, 512:1024]]
            Tqb, Tkb, Rqb = K.buf("Tq"), K.buf("Tk"), K.buf("Rq")
            alias([Tqb, Tkb, Rqb], [Tb, Rb, TNb, KSTb] + VSTb)
            Rkb = K.buf("Rk")
            alias([Rkb], RSb)
            ps_banks[0] = [0, 1, 2, 7]
            wstate = {}

            def prep_stages(c):
                d = BS[c % 2]
                chains = ((Tq, Tqb, Rq, Rqb), (Tk, Tkb, Rk, Rkb))
                pj, sqs = {}, {}
                oc = (c % 2) * 128

                steps = []

                def st0a():
                    if c % 2 == 0:
                        w, wqb = wslot()
                        wqk = w[:, 0:4096].rearrange("p (k a n) -> p k a n", k=8, a=2)
                        K.dma("pool", wqk[:, :, 0, :], wq[:, :, c * 128:(c + 2) * 128], writes=[wqb])
                        K.dma("pool", wqk[:, :, 1, :], wq[:, :, D + c * 128:D + (c + 2) * 128], writes=[wqb])
                        wstate["w"] = (wqk, wqb)
                    K.dma("sp", CKP, ck_d[j, :, c * 128:(c + 1) * 128].rearrange("(t p) f -> p t f", p=128), writes=[CKPb])
                    K.dma("pool", d["CVP"], cv_d[j, :, c * 128:(c + 1) * 128].rearrange("(t p) f -> p t f", p=128), writes=[d["CVPb"]])
                steps.append(st0a)

                def mk_proj(a, hf):
                    def f():
                        wqk, wqb = wstate["w"]
                        T, Tb_, R, Rb_ = chains[a]
                        sl = slice(hf * 512, (hf + 1) * 512)
                        p, pb = ps()
                        for k in range(8):
                            K.op("pe", lambda h, k=k: h.matmul(p[:], wqk[:, k, a, oc:oc + 128], HT[:, k, sl], start=(k == 0), stop=(k == 7)),
                                 reads=[wqb, HTb[k][hf]], writes=[pb], inc=(k == 7))
                        K.op("dve", lambda h: h.tensor_copy(T[:, sl], p[:]), reads=[pb], writes=[Tb_])
                    return f

                def mk_sq(a, hf):
                    def f():
                        T, Tb_, R, Rb_ = chains[a]
                        sl = slice(hf * 512, (hf + 1) * 512)
                        qi = sq_next[0]; sq_next[0] = (qi + 1) % 4
                        sqs[(a, hf)] = qi
                        K.op("act", lambda h: h.activation(out=SQ[qi][:], in_=T[:, sl], func=AF.Square), reads=[Tb_], writes=[SQb[qi]])
                    return f

                def mk_stat(a, hf):
                    def f():
                        T, Tb_, R, Rb_ = chains[a]
                        sl = slice(hf * 512, (hf + 1) * 512)
                        qi = sqs[(a, hf)]
                        p2, p2b = ps()
                        K.op("pe", lambda h: h.matmul(p2[:], BLKB[:], SQ[qi][:], start=True, stop=True), reads=[ABb, SQb[qi]], writes=[p2b])
                        K.op("act", lambda h: h.activation(out=R[:, sl], in_=p2[:], func=AF.Ln, bias=EPSV[:, 0:1]), reads=[p2b, EPb], writes=[Rb_])
                    return f

                def mk_rexp(a):
                    def f():
                        T, Tb_, R, Rb_ = chains[a]
                        K.op("act", lambda h: h.activation(out=R[:], in_=R[:], func=AF.Exp, scale=-0.5), reads=[Rb_], writes=[Rb_])
                    return f

                for a in range(2):
                    for hf in range(2):
                        steps.append(mk_proj(a, hf))
                for a in range(2):
                    for hf in range(2):
                        steps.append(mk_sq(a, hf))
                for a in range(2):
                    for hf in range(2):
                        steps.append(mk_stat(a, hf))
                steps.append(mk_rexp(0))
                steps.append(mk_rexp(1))
                steps.append(lambda: K.op("dve", lambda h: h.scalar_tensor_tensor(out=d["QTE"], in0=Tq[:], scalar=ACON[:, 193:194], in1=Rq[:], op0=ALU.mult, op1=ALU.mult), reads=[Tqb, Rqb, ACb], writes=[d["QTEb"]]))
                steps.append(lambda: K.op("dve", lambda h: h.scalar_tensor_tensor(out=Tk[:], in0=Tk[:], scalar=ACON[:, 194:195], in1=Rk[:], op0=ALU.mult, op1=ALU.mult), reads=[Tkb, Rkb, ACb], writes=[Tkb]))
                steps.append(lambda: K.op("act", lambda h: h.copy(d["QTO"], d["QTE"]), reads=[d["QTEb"]], writes=[d["QTOb"]]))
                steps.append(lambda: K.op("act", lambda h: h.copy(d["KE"][0:64, :], Tk[0:64, :]), reads=[Tkb], writes=[d["KEb"]]))
                steps.append(lambda: K.op("act", lambda h: h.copy(d["KO"][64:128, :], Tk[64:128, :]), reads=[Tkb], writes=[d["KOb"]]))

                def st3a():
                    K.dma("sp", d["QTE"][64:80, :], pmadd_d[:, :], reads=[d["QTOb"]], writes=[d["QTEb"]])
                    K.dma("sp", d["QTO"][0:16, :], pmadd_d[:, :], writes=[d["QTOb"]])
                steps.append(st3a)

                def mk_kt(hf):
                    def f():
                        p3, p3b = ps()
                        for t4 in range(4):
                            K.op("pe", lambda h, t4=t4: h.transpose(p3[:, t4 * 128:(t4 + 1) * 128], Tk[:, hf * 512 + t4 * 128:hf * 512 + (t4 + 1) * 128], IDENT[:]), reads=[Tkb, IDb], writes=[p3b], inc=(t4 == 3))
                        K.op("dve", lambda h: h.tensor_copy(KSTs[hf], p3[:]), reads=[p3b], writes=[Tqb])
                        K.dma("sp", nk_d[j, hf * 512:(hf + 1) * 512, c * 128:(c + 1) * 128].rearrange("(t p) f -> p t f", p=128), KSTs[hf].rearrange("p (t f) -> p t f", t=4), reads=[Tqb], final=True)
                    return f
                steps.append(mk_kt(0))
                steps.append(mk_kt(1))

                def st3c():
                    p4, p4b = ps()
                    for t4 in range(4):
                        K.op("pe", lambda h, t4=t4: h.transpose(p4[:, t4 * 128:(t4 + 1) * 128], CKP[:, t4, :], IDENT[:]), reads=[CKPb, IDb], writes=[p4b], inc=(t4 == 3))
                    K.op("dve", lambda h: h.tensor_copy(d["KCE"][0:64, :], p4[0:64, :]), reads=[p4b], writes=[d["KCEb"]])
                    K.op("dve", lambda h: h.tensor_copy(d["KCO"][64:128, :], p4[64:128, :]), reads=[p4b], writes=[d["KCOb"]])
                steps.append(st3c)

                def eb_dma(par):
                    def f():
                        hd = 2 * c + par
                        src = bass.AP(tensor=rpbr_t, offset=(j * 16 + hd) * 15 * 127, ap=[[1, 64], [127, 15], [1, 64]])
                        K.dma("sp", RP[0:64, :, :], src, writes=[RPb])
                        K.dma("sp", RP[64:128, :, :], src, writes=[RPb])
                    return f

                def eb(par):
                    def f():
                        K.op("act", lambda h: h.activation(out=E, in_=RP, func=AF.Exp), reads=[RPb], writes=[Eb])
                        K.op("dve", lambda h: h.tensor_tensor(out=E, in0=E, in1=ACON[:, 0:64].unsqueeze(1).broadcast_to([128, 15, 64]), op=ALU.mult), reads=[Eb, ACb], writes=[Eb])
                    return f

                def eb2(par, g):
                    def f():
                        Ef = reg_E
                        m0, nm = ((0, 8), (8, 7))[g]
                        p5, p5b = ps()
                        K.op("pe", lambda h: h.matmul(p5[:, 0:nm * 64], J2B[:], Ef[:, m0 * 64:(m0 + nm) * 64], start=True, stop=True), reads=[ABb, Eb], writes=[p5b])
                        K.op("dve", lambda h: h.tensor_copy(d["EBT"][par][0:64, 8 + m0:8 + m0 + nm, :], p5[0:64, 0:nm * 64].rearrange("p (m q) -> p m q", m=nm)), reads=[p5b], writes=[d["EBTb"][par]])
                        K.op("dve", lambda h: h.tensor_copy(d["EBT"][par][64:128, 9 + m0:9 + m0 + nm, :], p5[64:128, 0:nm * 64].rearrange("p (m q) -> p m q", m=nm)), reads=[p5b], writes=[d["EBTb"][par]])
                    return f
                steps.insert(1, eb_dma(0))
                k_mid = 1 + len(steps) // 2
                tail = steps[k_mid:]
                steps = steps[:k_mid] + [eb(0), eb_dma(1), eb2(0, 0), eb2(0, 1)] + tail + [eb(1), eb2(1, 0), eb2(1, 1)]
                return steps


            def attention(c, nxt):
                d = BS[c % 2]
                jobs = []
                for hf in range(2):
                    for par in range(2):
                        tl = [("ctx", t) for t in range(4)] + [("loc", r2) for r2 in (range(0, 6) if hf == 0 else range(2, 8))]
                        for idx, (kind_, t) in enumerate(tl):
                            jobs.append((hf, par, idx, kind_, t, idx == len(tl) - 1))
                LAG = 2
                state = {}

                def stageAB(n):
                    hf, par, idx, kind_, t, last = jobs[n]
                    sl = slice(hf * 512, (hf + 1) * 512)
                    Kx, Kxb = (d["KE"], d["KEb"]) if par == 0 else (d["KO"], d["KOb"])
                    KCx, KCxb = (d["KCE"], d["KCEb"]) if par == 0 else (d["KCO"], d["KCOb"])
                    Qx, Qxb = (d["QTE"], d["QTEb"]) if par == 0 else (d["QTO"], d["QTOb"])
                    sp_, spb = ps()
                    pi = n % 4
                    pt, ptb = PTt[pi], PTb[pi]
                    if kind_ == "ctx":
                        K.op("pe", lambda h: h.matmul(sp_[:], KCx[:, t * 128:(t + 1) * 128], Qx[:, sl], start=True, stop=True), reads=[KCxb, Qxb], writes=[spb])
                        K.op("act", lambda h: h.activation(out=pt, in_=sp_[:], func=AF.Exp, bias=ACON[:, 192:193]), reads=[spb, ACb], writes=[ptb])
                    else:
                        K.op("pe", lambda h: h.matmul(sp_[:], Kx[:, t * 128:(t + 1) * 128], Qx[:, sl], start=True, stop=True), reads=[Kxb, Qxb], writes=[spb])
                        K.op("act", lambda h: h.activation(out=pt, in_=sp_[:], func=AF.Exp), reads=[spb], writes=[ptb])
                        m0 = 15 - 2 * t + 8 * hf
                        pt3 = pt.rearrange("p (m q) -> p m q", m=8)
                        K.op("dve", lambda h: h.tensor_tensor(out=pt3, in0=pt3, in1=d["EBT"][par][:, m0:m0 + 8, :], op=ALU.mult), reads=[ptb, d["EBTb"][par]], writes=[ptb])
                    state[n] = (pt, ptb)

                def stageC(n):
                    hf, par, idx, kind_, t, last = jobs[n]
                    sl = slice(hf * 512, (hf + 1) * 512)
                    pt, ptb = state.pop(n)
                    Ops, Opb = PS[3 + par], PSb[3 + par]
                    Dps, Dpb = PS[5 + par], PSb[5 + par]
                    if kind_ == "ctx":
                        K.op("pe", lambda h: h.matmul(Ops[:], d["CVP"][:, t, :], pt, start=(idx == 0), stop=False), reads=[d["CVPb"], ptb], writes=[Opb], inc=False)
                    else:
                        K.op("pe", lambda h: h.matmul(Ops[:], VTOK[:, t, c * 128:(c + 1) * 128], pt, start=(idx == 0), stop=last), reads=[VTOKb[t], ptb], writes=[Opb], inc=False)
                    K.op("pe", lambda h: h.matmul(Dps[:], ONE1B[:], pt, start=(idx == 0), stop=last), reads=[ABb, ptb], writes=[Dpb, Opb], inc=True)
                    if last:
                        rows = slice(par * 64, par * 64 + 64)
                        K.op("act", lambda h: h.activation(out=RD[rows, :], in_=Dps[rows, :], func=AF.Ln), reads=[Dpb], writes=[RDb])
                        K.op("act", lambda h: h.activation(out=RD[rows, :], in_=RD[rows, :], func=AF.Exp, scale=-1.0), reads=[RDb], writes=[RDb])
                        K.op("dve", lambda h: h.tensor_tensor(out=OT[rows, c, sl], in0=Ops[rows, :], in1=RD[rows, :], op=ALU.mult), reads=[Opb, RDb], writes=[OTb[c][hf]])

                sched = {}
                for k_, st in enumerate(nxt):
                    sched.setdefault(int(k_ * 37 / max(1, len(nxt))), []).append(st)
                for n in range(len(jobs) + LAG):
                    for st in sched.get(n, ()):
                        st()
                    if n < len(jobs):
                        stageAB(n)
                    if n - LAG >= 0:
                        stageC(n - LAG)

            for st in prep_stages(0):
                st()
            for c in range(8):
                if astage < 3:
                    if c + 1 < 8:
                        for st in prep_stages(c + 1):
                            st()
                    continue
                if cfg.get("interleave", 1):
                    attention(c, prep_stages(c + 1) if c + 1 < 8 else [])
                else:
                    attention(c, [])
                    if c + 1 < 8:
                        for st in prep_stages(c + 1):
                            st()
            ps_banks[0] = list(range(7))
            if astage < 3:
                K.op("pool", lambda h: h.memset(reg["OT"], 0.0), writes=[b for l in OTb for b in l])
            out_proj(attn_w_o[j], 8, lambda k, sl: OT[:, k, sl], lambda k, hf: [OTb[k][hf]], None)
            alias(MTb, allm)
            alias(WKb, allw + [Tqb, Tkb, Rqb])
            alias(RSb, [Rkb])


        GMASK = sb("GMASK", (128, 256), F32)
        GMb = IDb
        K.dma("sp", GMASK[:, 0:128], maskf_d[:, :], writes=[GMb])
        K.dma("sp", GMASK[:, 128:256], maskb_d[:, :], writes=[GMb])
        ONEV = sb("ONEV", (128, 1), F32)
        OVb = K.buf("ONEV")
        K.op("dve", lambda h: h.memset(ONEV[:], 1.0), writes=[OVb])
        ONES256 = sb("ONES256", (128, 128), BF16)
        O256b = K.buf("ONES256")
        K.op("dve", lambda h: h.memset(ONES256[:], 1.0 / 256), writes=[O256b])

        def gla_mixer(i, j):
            norm_mod("A1", 0)
            K.op("dve", lambda h: h.memset(vt("g1b", 0, 8), 0.0), writes=[VTb])
            vec_rows(0, 0, gla_b_gk[j].rearrange("d (c p) -> (d c) p", p=128), 8)
            vec_rows(0, 8, gla_o_norm[j].rearrange("(c p) -> c p", p=128), 2)
            stg_to_vt(0, 10, ["gv"])
            K.op("dve", lambda h: h.tensor_scalar(out=vt("gv", 0, 8), in0=vt("gv", 0, 8), scalar1=-1.0, scalar2=None, op0=ALU.mult), reads=[VTb], writes=[VTb])
            names = ["VTOK", "OT", "QG", "KG", "KOTK", "ATT", "SBF", "AT", "W2P", "W1C"]
            sizes = [8192, 8192, 1024, 1024, 512, 512, 256, 1024, 1024, 256]
            reg = {}
            o = 0
            for nm, sz in zip(names, sizes):
                reg[nm] = MTflat[:, o:o + sz]
                o += sz
            assert o <= NPAIR * NT
            VTOK = reg["VTOK"].rearrange("p (t n) -> p t n", t=8)
            OT = reg["OT"].rearrange("p (c n) -> p c n", c=8)
            QG, KG, SBF, AT = reg["QG"], reg["KG"], reg["SBF"], reg["AT"]
            KOTK = [reg["KOTK"][:, a * 128:(a + 1) * 128] for a in range(4)]
            ATT = [reg["ATT"][:, a * 128:(a + 1) * 128] for a in range(4)]
            W2P = reg["W2P"].rearrange("p (d n) -> p d n", d=2)
            W1C = reg["W1C"].rearrange("p (k r) -> p k r", k=8)
            VTOKb = [K.buf("gVTOK%d" % t) for t in range(8)]
            OTb = [[K.buf("gOT%d_%d" % (c, h)) for h in range(2)] for c in range(8)]
            QGb, KGb, SBFb, ATb, W2Pb, W1Cb = [K.buf(n) for n in ("QG", "KG", "SBF", "AT", "W2P", "W1C")]
            KOTKb = [K.buf("KOTK%d" % a) for a in range(4)]
            ATTb = [K.buf("ATT%d" % a) for a in range(4)]
            allm = VTOKb + [b for l in OTb for b in l] + [QGb, KGb, SBFb, ATb, W2Pb, W1Cb] + KOTKb + ATTb
            alias(allm, MTb)
            QF, KF, GL, PM, TX, KOT = WK[0], WK[1], WK[2], WK[3], WK[4], WK[5]
            OF = [WK[6], WK[7]]
            QFb, KFb, GLb, PMb, TXb, KOTb = [K.buf(n) for n in ("QF", "KF", "GL", "PM", "TX", "KOT")]
            OFb = [[K.buf("OF%d_%d" % (vc, c)) for c in range(8)] for vc in range(2)]
            allw = [QFb, KFb, GLb, PMb, TXb, KOTb] + [b for l in OFb for b in l]
            alias(allw, WKb)
            Sst = RSTD[:, 512:768]
            DEC = RSTD[:, 768:776]
            RES = PAIR2[:, 0:NT]
            RESb, Sb, DECb = K.buf("RESET"), K.buf("S"), K.buf("DEC")
            alias([RESb], PAIR2_bufs)
            alias([Sb, DECb], [RSb[1]])
            K.op("pool", lambda h: h.memset(RES, 1.0), writes=[RESb])
            K.op("pool", lambda h: h.memset(RES.rearrange("p (c t) -> p c t", t=128)[:, :, 0:1], 0.0), reads=[RESb], writes=[RESb])
            K.op("pool", lambda h: h.memset(reg["W2P"], 0.0), writes=[W2Pb])
            for d in range(2):
                K.dma("pool", W1C[:, :, d * 16:(d + 1) * 16], gla_w_gk1[j, d].rearrange("(k p) r -> p k r", p=128), writes=[W1Cb])
                K.dma("pool", W2P[d * 16:(d + 1) * 16, d, :], gla_w_gk2[j, d], writes=[W2Pb])
            for hf in range(2):
                sl = slice(hf * 512, (hf + 1) * 512)
                p, pb = ps()
                for k in range(8):
                    K.op("pe", lambda h, k=k, p=p, sl=sl: h.matmul(p[0:32, :], W1C[:, k, :], HT[:, k, sl], start=(k == 0), stop=(k == 7)), reads=[W1Cb, HTb[k][hf]], writes=[pb], inc=(k == 7))
                K.op("act", lambda h, p=p, sl=sl: h.copy(AT[0:32, sl], p[0:32, :]), reads=[pb], writes=[ATb])
            wvd = gla_w_v[j].rearrange("(k p) n -> p k n", p=128)
            for ch in range(2):
                w, wb = wslot()
                wv = w[:, 0:4096].rearrange("p (k n) -> p k n", k=8)
                K.dma("pool", wv, wvd[:, :, ch * 512:(ch + 1) * 512], writes=[wb])
                for tt in range(8):
                    p, pb = ps()
                    for k in range(8):
                        K.op("pe", lambda h, k=k, p=p, wv=wv, tt=tt: h.matmul(p[:], HT[:, k, tt * 128:(tt + 1) * 128], wv[:, k, :], start=(k == 0), stop=(k == 7)),
                             reads=[wb, HTb[k][tt // 4]], writes=[pb], inc=(k == 7))
                    K.op("act", lambda h, p=p, tt=tt, ch=ch: h.copy(VTOK[:, tt, ch * 512:(ch + 1) * 512], p[:]), reads=[pb], writes=[VTOKb[tt]])
            gwq = gla_w_q[j].rearrange("(k p) n -> p k n", p=128)
            gwk = gla_w_k[j].rearrange("(k p) n -> p k n", p=128)
            wgd = gla_w_g[j].rearrange("(k p) n -> p k n", p=128)
            wgt, wgb = None, None
            QG2, KG2 = PAIR2[:, 1024:2048], PAIR2[:, 2048:3072]
            KOT2 = PAIR2[:, 3072:5120].bitcast(F32)
            DEC2 = RSTD[:, 776:784]
            QG2b, KG2b, KOT2b, DEC2b = K.buf("QG2"), K.buf("KG2"), K.buf("KOT2"), K.buf("DEC2")
            alias([QG2b, KG2b, KOT2b], PAIR2_bufs)
            alias([DEC2b], [DECb])
            SETS = [dict(QG=QG, QGb=QGb, KG=KG, KGb=KGb, KOT=KOT[:, :], KOTb=KOTb, DEC=DEC, DECb=DECb),
                    dict(QG=QG2, QGb=QG2b, KG=KG2, KGb=KG2b, KOT=KOT2, KOTb=KOT2b, DEC=DEC2, DECb=DEC2b)]
            wst = {}
            PM3 = PM[:, :].rearrange("p (c t) -> p c t", t=128)
            TX3 = TX[:, :].rearrange("p (c t) -> p c t", t=128)

            def setup_steps(hd, d):
                B = SETS[d]
                edge = 127 if d == 0 else 0
                steps = []
                if d == 0:
                    def ld():
                        w, wqb = wslot()
                        wqk = w[:, 0:2048].rearrange("p (k a n) -> p k a n", k=8, a=2)
                        K.dma("pool", wqk[:, :, 0, :], gwq[:, :, hd * 128:(hd + 1) * 128], writes=[wqb])
                        K.dma("pool", wqk[:, :, 1, :], gwk[:, :, hd * 128:(hd + 1) * 128], writes=[wqb])
                        wst["w"] = (wqk, wqb)
                    steps.append(ld)

                    def mk_qk(a, hf):
                        def f():
                            wqk, wqb = wst["w"]
                            dst, dstb, scl = ((QF, QFb, 128.0 ** -0.5), (KF, KFb, 1.0))[a]
                            sl = slice(hf * 512, (hf + 1) * 512)
                            p, pb = ps()
                            for k in range(8):
                                K.op("pe", lambda h, k=k: h.matmul(p[:], wqk[:, k, a, :], HT[:, k, sl], start=(k == 0), stop=(k == 7)),
                                     reads=[wqb, HTb[k][hf]], writes=[pb], inc=(k == 7))
                            K.op("act", lambda h: h.activation(out=dst[:, sl], in_=p[:], func=AF.Copy, scale=scl), reads=[pb], writes=[dstb])
                        return f
                    for a in range(2):
                        for hf in range(2):
                            steps.append(mk_qk(a, hf))

                def mk_gl(hf):
                    def f():
                        sl = slice(hf * 512, (hf + 1) * 512)
                        p, pb = ps()
                        K.op("pe", lambda h: h.matmul(p[:], W2P[0:32, d, hd * 128:(hd + 1) * 128], AT[0:32, sl], start=True, stop=True), reads=[W2Pb, ATb], writes=[pb])
                        K.op("act", lambda h: h.activation(out=TX[:, sl], in_=p[:], func=AF.Exp, scale=-1.0, bias=vt("gv", d * 4 + hd)), reads=[pb, VTb], writes=[TXb])
                        K.op("act", lambda h: h.activation(out=GL[:, sl], in_=TX[:, sl], func=AF.Ln, bias=ONEV[:, 0:1]), reads=[TXb, OVb], writes=[GLb])
                    return f
                steps.append(mk_gl(0))
                steps.append(mk_gl(1))
                steps.append(lambda: K.op("dve", lambda h: h.tensor_tensor_scan(out=PM[:], data0=RES, data1=GL[:], initial=0.0, op0=ALU.mult, op1=ALU.add), reads=[RESb, GLb], writes=[PMb]))
                if d == 1:
                    steps.append(lambda: K.op("dve", lambda h: h.tensor_tensor(out=TX3, in0=PM3[:, :, 127:128].broadcast_to([128, 8, 128]), in1=PM3, op=ALU.subtract), reads=[PMb], writes=[TXb]))
                    steps.append(lambda: K.op("dve", lambda h: h.tensor_tensor(out=PM[:], in0=TX[:], in1=GL[:], op=ALU.add), reads=[TXb, GLb], writes=[PMb]))
                steps.append(lambda: K.op("act", lambda h: h.activation(out=B["DEC"].rearrange("p (c o) -> p c o", o=1), in_=PM3[:, :, edge:edge + 1], func=AF.Exp, scale=-1.0 / 16), reads=[PMb], writes=[B["DECb"]]))
                steps.append(lambda: K.op("act", lambda h: h.activation(out=TX[:], in_=PM[:], func=AF.Exp, scale=-1.0 / 16), reads=[PMb], writes=[TXb]))
                steps.append(lambda: K.op("dve", lambda h: h.tensor_tensor(out=B["QG"], in0=QF[:], in1=TX[:], op=ALU.mult), reads=[QFb, TXb], writes=[B["QGb"]]))
                steps.append(lambda: K.op("act", lambda h: h.activation(out=TX[:], in_=PM[:], func=AF.Exp, scale=1.0 / 16), reads=[PMb], writes=[TXb]))
                steps.append(lambda: K.op("dve", lambda h: h.tensor_tensor(out=B["KG"], in0=KF[:], in1=TX[:], op=ALU.mult), reads=[KFb, TXb], writes=[B["KGb"]]))
                steps.append(lambda: K.op("dve", lambda h: h.tensor_tensor(out=TX3, in0=PM3, in1=PM3[:, :, edge:edge + 1].broadcast_to([128, 8, 128]), op=ALU.subtract), reads=[PMb], writes=[TXb]))
                steps.append(lambda: K.op("act", lambda h: h.activation(out=TX[:], in_=TX[:], func=AF.Exp, scale=1.0 / 16), reads=[TXb], writes=[TXb]))
                steps.append(lambda: K.op("dve", lambda h: h.tensor_tensor(out=B["KOT"], in0=KF[:], in1=TX[:], op=ALU.mult), reads=[KFb, TXb], writes=[B["KOTb"]]))
                return steps

            def chunks(hd, d, nxt):
                B = SETS[d]
                QGx, QGxb, KGx, KGxb, KOTx, KOTxb, DECx, DECxb = B["QG"], B["QGb"], B["KG"], B["KGb"], B["KOT"], B["KOTb"], B["DEC"], B["DECb"]
                K.dma("sp", Sst, (s0f_d if d == 0 else s0b_d)[hd], writes=[Sb])
                K.op("act", lambda h: h.copy(SBF, Sst), reads=[Sb], writes=[SBFb])
                order = list(range(8)) if d == 0 else list(range(7, -1, -1))

                def gA(ci):
                    c = order[ci]
                    cs = slice(c * 128, (c + 1) * 128)
                    ai = ci % 4
                    p, pb = ps()
                    K.op("pe", lambda h: h.matmul(p[:, 0:128], KGx[:, cs], QGx[:, cs], start=True, stop=True), reads=[KGxb, QGxb], writes=[pb])
                    K.op("dve", lambda h: h.tensor_tensor(out=ATT[ai], in0=p[:, 0:128], in1=GMASK[:, d * 128:(d + 1) * 128], op=ALU.mult), reads=[pb, GMb], writes=[ATTb[ai]])
                    p2, p2b = ps()
                    K.op("pe", lambda h: h.transpose(p2[:, 0:128], KOTx[:, cs], IDENT[:]), reads=[KOTxb, IDb], writes=[p2b])
                    K.op("act", lambda h: h.copy(KOTK[ai], p2[:, 0:128]), reads=[p2b], writes=[KOTKb[ai]])

                def gB(ci):
                    c = order[ci]
                    cs = slice(c * 128, (c + 1) * 128)
                    ai = ci % 4
                    p4, p4b = ps()
                    K.op("pe", lambda h: h.matmul(p4[:, 0:256], KOTK[ai], VTOK[:, c, hd * 256:(hd + 1) * 256], start=True, stop=True), reads=[KOTKb[ai], VTOKb[c]], writes=[p4b])
                    p3, p3b = ps()
                    for vc in range(2):
                        K.op("pe", lambda h, vc=vc: h.matmul(p3[:, vc * 128:(vc + 1) * 128], VTOK[:, c, hd * 256 + vc * 128:hd * 256 + (vc + 1) * 128], ATT[ai], start=True, stop=False),
                             reads=[VTOKb[c], ATTb[ai]], writes=[p3b], inc=False)
                        K.op("pe", lambda h, vc=vc: h.matmul(p3[:, vc * 128:(vc + 1) * 128], SBF[:, vc * 128:(vc + 1) * 128], QGx[:, cs], start=False, stop=True),
                             reads=[SBFb, QGxb], writes=[p3b], inc=(vc == 1))
                    K.op("dve", lambda h: h.scalar_tensor_tensor(out=Sst, in0=Sst, scalar=DECx[:, c:c + 1], in1=p4[:, 0:256], op0=ALU.mult, op1=ALU.add), reads=[Sb, DECxb, p4b], writes=[Sb])
                    seg_end = (c % 2 == 1) if d == 0 else (c % 2 == 0)
                    if seg_end:
                        K.dma("sp", (gsf_d if d == 0 else gsb_d)[c // 2, hd], Sst, reads=[Sb], final=True)
                        if ci < 7:
                            K.op("dve", lambda h: h.tensor_scalar(out=Sst, in0=Sst, scalar1=FLAG[:, 0:1], scalar2=None, op0=ALU.mult), reads=[Sb, FLb], writes=[Sb])
                    if ci < 7:
                        K.op("act", lambda h: h.copy(SBF, Sst), reads=[Sb], writes=[SBFb])
                    for vc in range(2):
                        if d == 0:
                            K.op("act", lambda h, vc=vc: h.copy(OF[vc][:, cs], p3[:, vc * 128:(vc + 1) * 128]), reads=[p3b], writes=[OFb[vc][c]])
                        else:
                            K.op("dve", lambda h, vc=vc: h.tensor_tensor(out=OF[vc][:, cs], in0=p3[:, vc * 128:(vc + 1) * 128], in1=OF[vc][:, cs], op=ALU.add), reads=[p3b, OFb[vc][c]], writes=[OFb[vc][c]])

                GLAG = 2
                nit = 8 + GLAG
                sched = {}
                for k_, st in enumerate(nxt):
                    sched.setdefault(min(nit - 1, int(k_ * nit / max(1, len(nxt)))), []).append(st)
                for ci in range(nit):
                    for st in sched.get(ci, ()):
                        st()
                    if ci < 8:
                        gA(ci)
                    if ci - GLAG >= 0:
                        gB(ci - GLAG)

            def onorm(hd):
                nonlocal_w = wst
                if hd % 2 == 0:
                    wgt, wgb = wslot()
                    wgv = wgt[:, 0:4096].rearrange("p (k n) -> p k n", k=8)
                    K.dma("pool", wgv, wgd[:, :, (hd // 2) * 512:(hd // 2 + 1) * 512], writes=[wgb])
                    wst["g"] = (wgv, wgb)
                wgv, wgb = wst["g"]
                for hf in range(2):
                    sl = slice(hf * 512, (hf + 1) * 512)
                    ofb = lambda vc: [OFb[vc][c] for c in range(hf * 4, hf * 4 + 4)]
                    p, pb = ps()
                    for vc in range(2):
                        qi = sq_next[0]; sq_next[0] = (qi + 1) % 4
                        K.op("act", lambda h, vc=vc, qi=qi: h.activation(out=SQ[qi][:], in_=OF[vc][:, sl], func=AF.Square), reads=ofb(vc), writes=[SQb[qi]])
                        K.op("pe", lambda h, p=p, vc=vc, qi=qi: h.matmul(p[:], ONES256[:], SQ[qi][:], start=(vc == 0), stop=(vc == 1)), reads=[O256b, SQb[qi]], writes=[pb])
                    K.op("act", lambda h, p=p: h.activation(out=RSTD[:, 0:512], in_=p[:], func=AF.Ln, bias=EPSV[:, 0:1]), reads=[pb, EPb], writes=[RSb[0]])
                    K.op("act", lambda h: h.activation(out=RSTD[:, 0:512], in_=RSTD[:, 0:512], func=AF.Exp, scale=-0.5), reads=[RSb[0]], writes=[RSb[0]])
                    for vc in range(2):
                        ch = hd * 2 + vc
                        oc = (ch % 4) * 128
                        pg, pgb = ps()
                        for k in range(8):
                            K.op("pe", lambda h, k=k, pg=pg, oc=oc, sl=sl: h.matmul(pg[:], wgv[:, k, oc:oc + 128], HT[:, k, sl], start=(k == 0), stop=(k == 7)), reads=[wgb, HTb[k][hf]], writes=[pg_b(pgb)], inc=(k == 7))
                        K.op("act", lambda h, pg=pg: h.activation(out=TX[:, 0:512], in_=pg[:], func=AF.Silu), reads=[pgb], writes=[TXb])
                        K.op("dve", lambda h, vc=vc: h.scalar_tensor_tensor(out=TX[:, 512:1024], in0=OF[vc][:, sl], scalar=vt("gv", 8 + vc), in1=RSTD[:, 0:512], op0=ALU.mult, op1=ALU.mult), reads=ofb(vc) + [VTb, RSb[0], TXb], writes=[TXb])
                        K.op("dve", lambda h, ch=ch: h.tensor_tensor(out=OT[:, ch, sl], in0=TX[:, 512:1024], in1=TX[:, 0:512], op=ALU.mult), reads=[TXb], writes=[OTb[ch][hf]])
            for st in setup_steps(0, 0):
                st()
            for u in range(8):
                hd, d = divmod(u, 2)
                nxt = setup_steps(*divmod(u + 1, 2)) if u + 1 < 8 else []
                chunks(hd, d, nxt)
                if d == 1:
                    onorm(hd)
            out_proj(gla_w_o[j], 8, lambda k, sl: OT[:, k, sl], lambda k, hf: [OTb[k][hf]], None)
            alias(MTb, allm)
            alias(WKb, allw)
            alias([RSb[1]], [Sb, DECb, DEC2b])

        def pg_b(b):
            return b

        mod_compute(0)
        for i in range(nlayers):
            layer_vectors(i)
            kind, j = i % 3, i // 3
            if kind in mixers:
                if kind == 1:
                    conv_mixer(i, j)
                if kind == 0:
                    attn_mixer(i, j)
                if kind == 2:
                    gla_mixer(i, j)
            ffn(i)

        K.finish()
    return nc


_CACHE = {}


def _run(inputs, cfg):
    key = repr(sorted(cfg.items()))
    if key not in _CACHE:
        _CACHE[key] = build_program(cfg)
    nc = _CACHE[key]
    f32 = lambda a: np.ascontiguousarray(np.asarray(a, dtype=np.float32))
    xp = f32(inputs["x_prompt"])
    xs = f32(inputs["x_sample"])
    c = f32(inputs["c"])
    cctx = f32(inputs["c_ctx"])
    shared = {n: f32(inputs[n]) for n in ("mod_w", "mod_b", "norm1_g", "norm2_g", "ffn_w_up", "ffn_b_up", "ffn_w_dw",
                                          "ffn_b_dw", "ffn_w_down", "ffn_b_down", "conv_w_pw1", "conv_b_pw1", "conv_w_dw", "conv_b_dw", "conv_ln_g", "conv_ln_b", "conv_w_pw2", "conv_b_pw2")}
    shared["ident"] = np.eye(128, dtype=np.float32)
    for n in ("attn_w_qkv", "attn_w_o", "attn_q_norm", "attn_k_norm", "gla_w_q", "gla_w_k", "gla_w_v", "gla_w_g", "gla_w_gk1", "gla_w_gk2", "gla_b_gk", "gla_o_norm", "gla_w_o"):
        shared[n] = f32(inputs[n])
    j64 = np.eye(64, dtype=np.float32)[::-1]
    j2 = np.zeros((128, 128), np.float32); j2[:64, :64] = j64; j2[64:, 64:] = j64
    blk = np.zeros((128, 128), np.float32); blk[:64, :64] = 1.0 / 64; blk[64:, 64:] = 1.0 / 64
    shared["j2"] = j2
    shared["blk"] = blk
    rpb = f32(inputs["attn_rpb"])
    rpbr_s = np.zeros((2, 16, 15, 127), np.float32)
    ys = np.arange(127)
    valid = (78 - ys >= 0) & (78 - ys <= 30)
    rpbr_s[:, :, :, valid] = rpb[:, :, ::-1, :][:, :, :, (78 - ys)[valid]]
    rpbr_p = np.zeros_like(rpbr_s)
    qc = np.arange(64)
    cstart = np.clip(qc - 8, 0, 48)
    kc_of_p = 63 - (np.arange(128) % 64)
    cm_s = ((kc_of_p[:, None] >= cstart[None, :]) & (kc_of_p[:, None] < cstart[None, :] + 16)).astype(np.float32)
    cm_p = np.ones((128, 64), np.float32)
    rho = 2 * np.arange(8)[None, :, None] + (np.arange(128) // 64)[:, None, None]
    jj = np.arange(16)[None, None, :]
    rs = np.clip(jj - 4, 0, 8)
    rr = np.arange(16)[:, None]
    jq = (np.arange(1024) // 64)[None, :]
    rsq = np.clip(jq - 4, 0, 8)
    pm_s = np.where((rr >= rsq) & (rr < rsq + 8), 0.0, -30000.0).astype(np.float32)
    pm_p = np.where((rr // 4) == (jq // 4), 0.0, -30000.0).astype(np.float32)
    import ml_dtypes
    shared["rowsel"] = (np.arange(1024)[None, :] // 64 == np.arange(16)[:, None]).astype(np.float32).astype(ml_dtypes.bfloat16)
    pm_s = pm_s.astype(ml_dtypes.bfloat16); pm_p = pm_p.astype(ml_dtypes.bfloat16)
    tri = np.tril(np.ones((128, 128), np.float32))
    shared["maskf"] = np.ascontiguousarray(tri.T)
    shared["maskb"] = np.ascontiguousarray(tri)
    sgf = f32(inputs["state_gla_fwd"])[:, 0]
    sgb = f32(inputs["state_gla_bwd"])[:, 0]
    zs = np.zeros((4, 128, 256), np.float32)
    ck = f32(inputs["cache_attn_k"]).reshape(4, 2, 512, D)
    cv = f32(inputs["cache_attn_v"]).reshape(4, 2, 512, D)
    zc = np.zeros((2, 512, D), np.float32)
    in_maps = []
    for core in range(8):
        m = dict(shared)
        if core < 4:
            m["xin"] = np.ascontiguousarray(xp[core * 4:(core + 1) * 4].reshape(NT, D))
            m["cond"] = np.ascontiguousarray(cctx.reshape(8, 128))
            m["flag"] = np.zeros((128, 1), np.float32)
            m["rpbr"] = rpbr_p; m["cmask"] = cm_p; m["pmadd"] = pm_p
            m["cmk"] = np.full((128, 1), -30000.0, np.float32)
            m["ck"] = zc; m["cv"] = zc
            m["s0f"] = zs; m["s0b"] = zs
        else:
            m["xin"] = np.ascontiguousarray(xs[core - 4])
            m["cond"] = np.ascontiguousarray(c[core - 4].reshape(8, 128))
            m["flag"] = np.ones((128, 1), np.float32)
            m["rpbr"] = rpbr_s; m["cmask"] = cm_s; m["pmadd"] = pm_s
            m["cmk"] = np.zeros((128, 1), np.float32)
            m["ck"] = np.ascontiguousarray(ck[core - 4]); m["cv"] = np.ascontiguousarray(cv[core - 4])
            m["s0f"] = np.ascontiguousarray(sgf[core - 4]); m["s0b"] = np.ascontiguousarray(sgb[core - 4])
        in_maps.append(m)
    res = run_bass_kernel_spmd(nc, in_maps, core_ids=list(range(8)))
    return res.results


def kernel(**inputs):
    cfg = {"mixers": (0, 1, 2), "nlayers": DEPTH, "astage": 3}
    r = _run(inputs, cfg)
    y_prompt = np.stack([r[cidx]["yout"] for cidx in range(4)], 0).reshape(16, 256, D)
    y_sample = np.stack([r[cidx]["yout"] for cidx in range(4, 8)], 0)
    nk = np.stack([r[cidx]["nk"] for cidx in range(4)], 0)
    nv = np.stack([r[cidx]["nv"] for cidx in range(4)], 0)
    new_k = nk.reshape(4, 2, 4, 256, 16, 64).transpose(0, 2, 1, 3, 4, 5).reshape(16, 2, 256, 16, 64)
    new_v = nv.reshape(4, 2, 4, 256, 16, 64).transpose(0, 2, 1, 3, 4, 5).reshape(16, 2, 256, 16, 64)
    new_sf = np.stack([r[cidx]["gsf"] for cidx in range(4)], 0).reshape(16, 1, 4, 128, 256)
    new_sb = np.stack([r[cidx]["gsb"] for cidx in range(4)], 0).reshape(16, 1, 4, 128, 256)
    f = lambda a: np.ascontiguousarray(a, dtype=np.float32)
    return (f(y_prompt), f(y_sample), f(new_k), f(new_v), f(new_sf), f(new_sb))
```
